# Optimizing a Trainium2 kernel written in Bass

```python
import math
import jax, jax.numpy as jnp
from jax import lax
import numpy as np


D_MODEL = 1024
BATCH = 1
SEQ = 16384
DEPTH = 2
DEC_BATCH = 128
DEC_SEQ = 4
PAST_LEN = 16384
PAGE_SIZE = 128

N_MIXERS = 2
N_CONV_LAYERS = (DEPTH + 1) // 2
N_ATTN_LAYERS = DEPTH // 2
CONV_WIDTH = 3
CONV_STATE = CONV_WIDTH - 1
HEAD_DIM = 64
N_HEADS = D_MODEL // HEAD_DIM
N_KV_HEADS = 2
GROUP = N_HEADS // N_KV_HEADS
WINDOW = 128
BLOCK = WINDOW
N_BUCKETS = 32
MAX_DISTANCE = 128
D_FF = -(-8 * D_MODEL // (3 * 256)) * 256
EPS = 1e-5
NEG_INF = -1e30
ATTN_SCALE = 1.0 / math.sqrt(HEAD_DIM)

kernel_name = 'hybrid_shortconv_swa_sink_decoder_step'


def rms_norm(x, g):
    xf = x.astype(jnp.float32)
    y = xf * lax.rsqrt(jnp.mean(xf * xf, axis=-1, keepdims=True) + EPS)
    return (y * g.astype(jnp.float32)).astype(x.dtype)


def swiglu(x, w_gate, w_up, w_down):
    return (jax.nn.silu(x @ w_gate) * (x @ w_up)) @ w_down


def t5_bucket(dist):
    n = jnp.maximum(dist, 0)
    max_exact = N_BUCKETS // 2
    nf = jnp.maximum(n, 1).astype(jnp.float32)
    large = max_exact + (jnp.log(nf / max_exact) / math.log(MAX_DISTANCE / max_exact)
                         * (N_BUCKETS - max_exact)).astype(jnp.int32)
    large = jnp.minimum(large, N_BUCKETS - 1)
    return jnp.where(n < max_exact, n, large)


def rel_bias(dist, rel_table):
    b = rel_table[t5_bucket(dist)].astype(jnp.float32)
    return jnp.transpose(b, (2, 0, 1)).reshape(N_KV_HEADS, GROUP, dist.shape[0], dist.shape[1])


def sink_softmax(logits, sinks):
    s = jnp.broadcast_to(sinks.astype(jnp.float32).reshape(N_KV_HEADS, GROUP, 1, 1),
                         logits.shape[:-1] + (1,))
    p = jax.nn.softmax(jnp.concatenate([logits, s], axis=-1), axis=-1)
    return p[..., :-1]


def conv_mixer(h, prev, w_conv_in, conv_w, w_conv_out):
    T = h.shape[1]
    b, c, xi = jnp.split(h @ w_conv_in, 3, axis=-1)
    u = c * xi
    up = jnp.concatenate([prev.astype(u.dtype), u], axis=1)
    v = conv_w[0] * up[:, 0:T] + conv_w[1] * up[:, 1:T + 1] + conv_w[2] * up[:, 2:T + 2]
    y = (b * v) @ w_conv_out
    return y, up[:, -CONV_STATE:]


def swa_prompt(h, w_q, w_k, w_v, w_o, sinks, rel_table):
    N, S, _ = h.shape
    nb = S // BLOCK
    q = (h @ w_q).reshape(N, nb, BLOCK, N_KV_HEADS, GROUP, HEAD_DIM)
    k = (h @ w_k).reshape(N, S, N_KV_HEADS, HEAD_DIM)
    v = (h @ w_v).reshape(N, S, N_KV_HEADS, HEAD_DIM)
    pad = jnp.zeros((N, BLOCK, N_KV_HEADS, HEAD_DIM), k.dtype)
    kp = jnp.concatenate([pad, k], axis=1).reshape(N, nb + 1, BLOCK, N_KV_HEADS, HEAD_DIM)
    vp = jnp.concatenate([pad, v], axis=1).reshape(N, nb + 1, BLOCK, N_KV_HEADS, HEAD_DIM)
    kb = jnp.concatenate([kp[:, :-1], kp[:, 1:]], axis=2)
    vb = jnp.concatenate([vp[:, :-1], vp[:, 1:]], axis=2)
    qi = jnp.arange(BLOCK)[:, None] + BLOCK
    kj = jnp.arange(2 * BLOCK)[None, :]
    dist = qi - kj
    band = (dist >= 0) & (dist <= WINDOW)
    blk = jnp.arange(nb)[:, None, None]
    valid = band[None] & ((blk > 0) | (kj >= BLOCK)[None])
    bias = rel_bias(dist, rel_table)
    logits = jnp.einsum('nbqkgd,nbskd->nbkgqs', q, kb,
                        preferred_element_type=jnp.float32) * ATTN_SCALE + bias
    logits = jnp.where(valid[None, :, None, None], logits, NEG_INF)
    p = sink_softmax(logits, sinks)
    o = jnp.einsum('nbkgqs,nbskd->nbqkgd', p.astype(vb.dtype), vb)
    y = o.reshape(N, S, D_MODEL) @ w_o
    return y, k[:, -WINDOW:], v[:, -WINDOW:]


def swa_sample(h, k_prev, v_prev, w_q, w_k, w_v, w_o, sinks, rel_table):
    N, T, _ = h.shape
    q = (h @ w_q).reshape(N, T, N_KV_HEADS, GROUP, HEAD_DIM)
    k_new = (h @ w_k).reshape(N, T, N_KV_HEADS, HEAD_DIM)
    v_new = (h @ w_v).reshape(N, T, N_KV_HEADS, HEAD_DIM)
    kc = jnp.concatenate([k_prev.astype(k_new.dtype), k_new], axis=1)
    vc = jnp.concatenate([v_prev.astype(v_new.dtype), v_new], axis=1)
    dist = jnp.arange(T)[:, None] - (jnp.arange(WINDOW + T)[None, :] - WINDOW)
    valid = (dist >= 0) & (dist <= WINDOW)
    bias = rel_bias(dist, rel_table)
    logits = jnp.einsum('nqkgd,nskd->nkgqs', q, kc,
                        preferred_element_type=jnp.float32) * ATTN_SCALE + bias
    logits = jnp.where(valid, logits, NEG_INF)
    p = sink_softmax(logits, sinks)
    o = jnp.einsum('nkgqs,nskd->nqkgd', p.astype(vc.dtype), vc)
    y = o.reshape(N, T, D_MODEL) @ w_o
    return y, kc[:, -WINDOW:], vc[:, -WINDOW:]


def trunk(x, conv_prev, k_prev, v_prev, g_mix, g_ffn, g_final, w_conv_in, conv_w, w_conv_out,
          w_q, w_k, w_v, w_o, sinks, rel_table, w_gate, w_up, w_down):
    conv_states, k_states, v_states = [], [], []
    for i in range(DEPTH):
        j = i // N_MIXERS
        h = rms_norm(x, g_mix[i])
        if i % N_MIXERS == 0:
            y, st = conv_mixer(h, conv_prev[j], w_conv_in[j], conv_w[j], w_conv_out[j])
            conv_states.append(st)
        else:
            if k_prev is None:
                y, ks, vs = swa_prompt(h, w_q[j], w_k[j], w_v[j], w_o[j], sinks[j], rel_table)
            else:
                y, ks, vs = swa_sample(h, k_prev[j], v_prev[j], w_q[j], w_k[j], w_v[j], w_o[j],
                                       sinks[j], rel_table)
            k_states.append(ks)
            v_states.append(vs)
        x = x + y
        x = x + swiglu(rms_norm(x, g_ffn[i]), w_gate[i], w_up[i], w_down[i])
    return rms_norm(x, g_final), jnp.stack(conv_states), jnp.stack(k_states), jnp.stack(v_states)


def setup_inputs(seed: int = 0) -> dict:
    key = jax.random.key(seed)
    ks = jax.random.split(key, 24)
    nrm = lambda k, shape, s: jax.random.normal(k, shape, jnp.float32) * s
    D = D_MODEL
    return {
        'x_prompt': nrm(ks[0], (BATCH, SEQ, D), 1.0),
        'x_sample': nrm(ks[1], (DEC_BATCH, DEC_SEQ, D), 1.0),
        'state_conv': nrm(ks[2], (N_CONV_LAYERS, DEC_BATCH, CONV_STATE, D), 1.0),
        'cache_k': nrm(ks[3], (N_ATTN_LAYERS, DEC_BATCH, WINDOW, N_KV_HEADS, HEAD_DIM), 1.0),
        'cache_v': nrm(ks[4], (N_ATTN_LAYERS, DEC_BATCH, WINDOW, N_KV_HEADS, HEAD_DIM), 1.0),
        'g_mix': 1.0 + nrm(ks[5], (DEPTH, D), 0.05),
        'g_ffn': 1.0 + nrm(ks[6], (DEPTH, D), 0.05),
        'g_final': 1.0 + nrm(ks[7], (D,), 0.05),
        'w_conv_in': nrm(ks[8], (N_CONV_LAYERS, D, 3 * D), D ** -0.5),
        'conv_w': nrm(ks[9], (N_CONV_LAYERS, CONV_WIDTH, D), CONV_WIDTH ** -0.5),
        'w_conv_out': nrm(ks[10], (N_CONV_LAYERS, D, D), D ** -0.5),
        'w_q': nrm(ks[11], (N_ATTN_LAYERS, D, N_HEADS * HEAD_DIM), D ** -0.5),
        'w_k': nrm(ks[12], (N_ATTN_LAYERS, D, N_KV_HEADS * HEAD_DIM), D ** -0.5),
        'w_v': nrm(ks[13], (N_ATTN_LAYERS, D, N_KV_HEADS * HEAD_DIM), D ** -0.5),
        'w_o': nrm(ks[14], (N_ATTN_LAYERS, N_HEADS * HEAD_DIM, D), (N_HEADS * HEAD_DIM) ** -0.5),
        'sinks': nrm(ks[15], (N_ATTN_LAYERS, N_HEADS), 0.5),
        'rel_table': nrm(ks[16], (N_BUCKETS, N_HEADS), 0.5),
        'w_gate': nrm(ks[17], (DEPTH, D, D_FF), D ** -0.5),
        'w_up': nrm(ks[18], (DEPTH, D, D_FF), D ** -0.5),
        'w_down': nrm(ks[19], (DEPTH, D_FF, D), D_FF ** -0.5),
    }


def reference(x_prompt, x_sample, state_conv, cache_k, cache_v, g_mix, g_ffn, g_final,
              w_conv_in, conv_w, w_conv_out, w_q, w_k, w_v, w_o, sinks, rel_table,
              w_gate, w_up, w_down):
    zero_prev = jnp.zeros((N_CONV_LAYERS, x_prompt.shape[0], CONV_STATE, D_MODEL), x_prompt.dtype)
    y_prompt, state_conv_prompt, cache_k_prompt, cache_v_prompt = trunk(
        x_prompt, zero_prev, None, None, g_mix, g_ffn, g_final, w_conv_in, conv_w, w_conv_out,
        w_q, w_k, w_v, w_o, sinks, rel_table, w_gate, w_up, w_down)
    y_sample, state_conv_sample, cache_k_sample, cache_v_sample = trunk(
        x_sample, state_conv, cache_k, cache_v, g_mix, g_ffn, g_final, w_conv_in, conv_w, w_conv_out,
        w_q, w_k, w_v, w_o, sinks, rel_table, w_gate, w_up, w_down)
    return (y_prompt, y_sample, state_conv_prompt, state_conv_sample,
            cache_k_prompt, cache_k_sample, cache_v_prompt, cache_v_sample)
```

```python
import math
import numpy as np
import concourse.bass as bass
import concourse.mybir as mybir
from concourse.bass_utils import run_bass_kernel_spmd

F32 = mybir.dt.float32
BF16 = mybir.dt.bfloat16
AF = mybir.ActivationFunctionType
ALU = mybir.AluOpType

NCORES = 8
D = 1024
NCH = 8
DFF = 2816
NFF = 22
NT = 2242
C_H0 = 2
C_P0 = 130
C_S0 = 2178
NQ = NT - C_P0
EPS = 1e-5
FF_GROUPS = [list(range(0, 8)), list(range(8, 15)), list(range(15, 22))]
NW = 8


def split(c0, c1, n=5):
    tot = c1 - c0
    base = tot // n
    rem = tot % n
    out = []
    c = c0
    for i in range(n):
        w = base + (1 if i < rem else 0)
        out.append((c, c + w))
        c += w
    return out


class Sched:
    def __init__(self):
        self.ops = []
        self.regions = {}

    def _recs(self, rg):
        return self.regions.setdefault(rg, {})

    def emit(self, eng, fn, reads=(), writes=(), dma=None):
        op = dict(id=len(self.ops), eng=eng, fn=fn, deps=set(), dma=dma, signal=False)
        for (rg, b0, b1) in reads:
            recs = self._recs(rg)
            for (k0, k1), rec in recs.items():
                if k0 < b1 and b0 < k1 and rec[0] is not None:
                    op['deps'].add(rec[0])
            rec = recs.get((b0, b1))
            if rec is None:
                rec = [None, {}]
                recs[(b0, b1)] = rec
            rec[1][eng if dma is None else ('dma', op['id'])] = op['id']
        for (rg, b0, b1) in writes:
            recs = self._recs(rg)
            dele = []
            for (k0, k1), rec in recs.items():
                if k0 < b1 and b0 < k1:
                    if rec[0] is not None:
                        op['deps'].add(rec[0])
                    for r in rec[1].values():
                        op['deps'].add(r)
                    if b0 <= k0 and k1 <= b1:
                        dele.append((k0, k1))
            for k in dele:
                del recs[k]
            recs[(b0, b1)] = [op['id'], {}]
        op['deps'].discard(op['id'])
        self.ops.append(op)
        return op

    def finalize(self, nc, final_groups):
        ops = self.ops
        engs = ['pe', 'act', 'dve', 'pool', 'sp']
        for op in ops:
            for d in op['deps']:
                dop = ops[d]
                if dop['dma'] is not None:
                    continue
                if dop['eng'] == 'pe' and op['eng'] == 'pe' and op['dma'] is None:
                    continue
                dop['signal'] = True
        cnt = {e: 0 for e in engs}
        gcnt = {}
        for op in ops:
            if op['dma'] is not None:
                g = op['dma']
                gcnt[g] = gcnt.get(g, 0) + 16
                op['sig'] = (('g', g), gcnt[g])
            elif op['signal']:
                cnt[op['eng']] += 1
                op['sig'] = (('e', op['eng']), cnt[op['eng']])
        self.gtotal = dict(gcnt)
        sem_keys = [('e', e) for e in engs] + [('g', g) for g in gcnt]
        return sem_keys

    def run(self, nc, sems, final_groups, wait_all_groups=()):
        ops = self.ops
        per = {e: [] for e in ['pe', 'act', 'dve', 'pool', 'sp']}
        for op in ops:
            per[op['eng']].append(op)
        gtotal = self.gtotal

        def body(eng, h):
            waited = {}
            for op in per[eng]:
                need = {}
                for d in op['deps']:
                    dop = ops[d]
                    if dop['dma'] is None:
                        if dop['eng'] == 'pe' and eng == 'pe' and op['dma'] is None:
                            continue
                    key, val = dop['sig']
                    if key[0] == 'g' and key[1] in wait_all_groups:
                        val = gtotal[key[1]]
                    if need.get(key, 0) < val:
                        need[key] = val
                for key, val in need.items():
                    if waited.get(key, 0) >= val:
                        continue
                    h.wait_ge(sems[key], val)
                    waited[key] = val
                ins = op['fn'](h)
                if op['dma'] is not None:
                    ins.then_inc(sems[op['sig'][0]], 16)
                elif op['signal']:
                    ins.then_inc(sems[op['sig'][0]], 1)
            if eng == 'sp':
                for g in final_groups:
                    if g in gtotal:
                        h.wait_ge(sems[('g', g)], gtotal[g])
        return body


class Buf:
    def __init__(self, nc, name, fshape, dtype, region, roff, boff):
        self.t = nc.alloc_sbuf_tensor_at(name, [128] + list(fshape), dtype, offset=roff + boff)
        self.es = 4 if dtype == F32 else 2
        self.fshape = list(fshape)
        self.region = region
        self.rb = boff
        self.C = fshape[-1]
        self.nbytes = self.es * int(np.prod(fshape))

    def iv(self, planes, c0, c1):
        lin = 0
        for dsz, i in zip(self.fshape[:-1], planes):
            lin = lin * dsz + i
        b = self.rb + lin * self.C * self.es
        return (self.region, b + c0 * self.es, b + c1 * self.es)

    def all(self):
        return (self.region, self.rb, self.rb + self.nbytes)


def t5_bucket_np(dist):
    n = np.maximum(dist, 0)
    max_exact = 16
    nf = np.maximum(n, 1).astype(np.float32)
    large = max_exact + (np.log(nf / np.float32(max_exact)) / np.float32(math.log(128 / max_exact))
                         * np.float32(32 - max_exact)).astype(np.int32)
    large = np.minimum(large, 31)
    return np.where(n < max_exact, n, large)


def build_program(max_ops=None, marks=None):
    nc = bass.Bass("TRN2", target_bir_lowering=False, dynamic_dma_scratch_size=8192)
    S = Sched()
    MK = {}

    def dram_in(name, shape):
        return nc.dram_tensor(name, list(shape), F32, kind="ExternalInput").ap()

    def dram_out(name, shape):
        return nc.dram_tensor(name, list(shape), F32, kind="ExternalOutput").ap()

    xT = dram_in("xT", [D, NT])
    scT = dram_in("scT", [D, 16, 2])
    ck = dram_in("ck", [16, 128, 128])
    cv = dram_in("cv", [16, 128, 128])
    gall = dram_in("gall", [128, 40])
    cwd = dram_in("cwd", [128, 24])
    sinks = dram_in("sinks", [1, 16])
    rel = dram_in("rel", [128, 16])
    onehot = dram_in("onehot", [128, 384])
    ident_d = dram_in("ident", [128, 128])
    cmask_d = dram_in("cmask", [128, 1])
    w_conv_in = dram_in("w_conv_in", [D, 3 * D])
    w_conv_out = dram_in("w_conv_out", [D, D])
    w_q = dram_in("w_q", [D, D])
    wkpad = dram_in("wkpad", [4, D, 128])
    w_k = dram_in("w_k", [D, 128])
    w_v = dram_in("w_v", [D, 128])
    w_o = dram_in("w_o", [D, D])
    w_gate = dram_in("w_gate", [2, D, DFF])
    w_up = dram_in("w_up", [2, D, DFF])
    w_down = dram_in("w_down", [2, DFF, D])

    yT = dram_out("yT", [D, NQ])
    uT = dram_out("uT", [D, 66])
    kvp = dram_out("kvp", [2, 128, 128])
    cks = dram_out("cks", [16, 128, 128])
    cvs = dram_out("cvs", [16, 128, 128])
    Bscr_t = nc.dram_tensor("Bscr", [16, 128, 384], F32, kind="Internal")
    Bscr = Bscr_t.ap()

    base = 8320
    regions = {}
    cur = [base]

    def region(name, size):
        size = (size + 31) // 32 * 32
        regions[name] = (cur[0], size)
        cur[0] += size
        return cur[0] - size

    rX = region('X', NCH * NT * 4)
    rH = region('H', NCH * NT * 2)
    rS = region('S', NCH * NT * 2)
    rW = region('W', NW * 2048)
    rT = region('T', 18976)
    rN = region('N', 6336)
    rV = region('V', 18 * 320 * 2 + 64)
    rB = region('B', 3 * 4096)
    rC = region('C', 10496)
    assert cur[0] <= 229344, cur[0]

    def mk(name, fshape, dtype, rname, boff):
        roff = regions[rname][0]
        b = Buf(nc, name, fshape, dtype, rname, roff, boff)
        assert boff + b.nbytes <= regions[rname][1], (name, boff, b.nbytes, regions[rname])
        return b

    X = mk("X", [NCH, NT], F32, 'X', 0)
    H = mk("H", [NCH, NT], BF16, 'H', 0)
    SB = mk("SB", [NCH, NT], BF16, 'S', 0)
    WR = mk("WR", [NW, 8, 128], BF16, 'W', 0)
    UB = mk("UB", [2, 2180], BF16, 'T', 0)
    US = mk("US", [2, 16, 6], BF16, 'T', 8736)
    CSB = mk("CSB", [2, 450], F32, 'T', 9120)
    BSB = mk("BSB", [3, 450], F32, 'T', 12736)
    KP = mk("KP", [4, NT], BF16, 'T', 0)
    SQ = mk("SQ", [5, 450], BF16, 'N', 0)
    TMP = mk("TMP", [1, 450], F32, 'N', 4512)
    VA = mk("VA", [18, 320], BF16, 'V', 0)
    ACC = mk("ACC", [3, 450], F32, 'V', 0)
    SG = mk("SG", [3, 450], F32, 'V', 6144)
    TB = mk("TB", [3, 16, 128], BF16, 'B', 0)
    off = [0]

    def cmk(name, fshape, dtype):
        b = mk(name, fshape, dtype, 'C', off[0])
        off[0] += (b.nbytes + 31) // 32 * 32
        return b

    GA = cmk("GA", [40], F32)
    CW = cmk("CW", [24], F32)
    ONES = cmk("ONES", [128], BF16)
    IDB = cmk("IDB", [128], BF16)
    ESEL = cmk("ESEL", [2, 128], BF16)
    ESR = cmk("ESR", [16, 128], BF16)
    CM = cmk("CM", [1], F32)
    U32 = cmk("U32", [NCH, 66], F32)
    SCB = cmk("SCB", [NCH, 16, 2], BF16)
    RELT = cmk("RELT", [16], F32)
    EF = cmk("EF", [16], F32)
    EHI = cmk("EHI", [16], BF16)
    ELO = cmk("ELO", [16], BF16)
    SNK = cmk("SNK", [16], F32)
    ESN = cmk("ESN", [16], F32)
    KST = cmk("KST", [2, 128], F32)
    VST = cmk("VST", [2, 128], F32)
    EBC = mk("EBC", [2, 16, 128], BF16, 'S', 13472)
    REP = mk("REP", [8, 384], F32, 'S', 21664)
    OHB = mk("OHB", [384], BF16, 'S', 33952)
    PB = mk("PB", [4, 512], BF16, 'H', 0)
    REC = mk("REC", [2, 512], F32, 'H', 6144)
    KC2 = mk("KC2", [16, 2, 128], BF16, 'H', 10240)
    VAS = mk("VAS", [16, 320], BF16, 'H', 18432)
    KCT = mk("KCT", [4, 4, 128], BF16, 'N', 2048)
    OWN = mk("OWN", [4, 16, 4, 4], F32, 'H', 30720)
    PP0 = mk("PP0", [640], BF16, 'H', 4096)
    PP1 = mk("PP1", [640], BF16, 'H', 28672)
    PO = mk("PO", [512], BF16, 'H', 34816)
    RCS = mk("RCS", [1, 256], F32, 'H', 30720)
    YST = mk("YST", [2, NCH, 424], F32, 'H', 0)
    OWNB = mk("OWNB", [1024], BF16, 'N', 0)

    ps_t = nc.alloc_psum_tensor("ps", [128, 8, 512], F32)

    def psiv(bank, c0, c1):
        return ('ps', bank * 2048, (bank + 1) * 2048)

    bank_ctr = [0]

    def nb():
        b = bank_ctr[0] % 8
        bank_ctr[0] += 1
        return b

    ps = ps_t

    pieces = []

    def wview(ap2d, col0):
        return ap2d.rearrange("(k p) n -> p k n", p=128)[:, :, col0:col0 + 128]

    wstate = dict(loaded=0)

    def add_piece(src, npl=8):
        pieces.append((src, npl))
        return len(pieces) - 1

    def ensure_loaded(upto):
        upto = min(upto, len(pieces) - 1)
        while wstate['loaded'] <= upto:
            n = wstate['loaded']
            src, npl = pieces[n]
            slot = n % NW
            dst = WR.t[:, slot, 0:npl, :]
            S.emit('pool', (lambda d, s: (lambda e: e.dma_start(out=d, in_=s)))(dst, src),
                   writes=[WR.iv((slot, 0), 0, npl * 128)], dma='w%d' % slot)
            wstate['loaded'] += 1

    piece_ctr = [0]

    def use_piece():
        n = piece_ctr[0]
        piece_ctr[0] += 1
        assert n < wstate['loaded'], (n, wstate['loaded'])
        return n % NW

    def release_upto(n):
        ensure_loaded(n + NW)

    def release_all():
        ensure_loaded(piece_ctr[0] - 1 + NW)

    for m in range(NCH):
        add_piece(wview(w_conv_in, D + m * 128))
        add_piece(wview(w_conv_in, 2 * D + m * 128))
        add_piece(wview(w_conv_in, m * 128))
    for m in range(NCH):
        add_piece(wview(w_conv_out, m * 128))

    def ffn_pieces(l):
        for grp in FF_GROUPS:
            for j in grp:
                add_piece(wview(w_gate[l], j * 128))
                add_piece(wview(w_up[l], j * 128))
            for m in range(NCH):
                src = w_down[l][grp[0] * 128:(grp[-1] + 1) * 128, m * 128:(m + 1) * 128].rearrange(
                    "(jj p) c -> p jj c", p=128)
                add_piece(src, len(grp))
    ffn_pieces(0)
    for v in range(4):
        add_piece(wview(wkpad[v], 0))
    add_piece(wview(w_v, 0))
    add_piece(wview(w_k, 0))
    for m in range(NCH):
        add_piece(wview(w_q, m * 128))
    for m in range(NCH):
        add_piece(wview(w_o, m * 128))
    ffn_pieces(1)

    class _Rec:
        def __getattr__(self, name):
            def f(*a, **k):
                self.call = (name, a, k)
                return self
            return f

    def E(eng, fn, reads=(), writes=(), dma=None):
        r = _Rec()
        fn(r)
        name, a, k = r.call
        return S.emit(eng, lambda h: getattr(h, name)(*a, **k), reads, writes, dma)

    def mm(out, lhsT, rhs, start, stop, reads, writes):
        E('pe', lambda e: e.matmul(out, lhsT=lhsT, rhs=rhs, start=start, stop=stop), reads, writes)

    def cload(dst_buf, dst_ap, src_ap, eng='sp'):
        E(eng, lambda e: e.dma_start(out=dst_ap, in_=src_ap), writes=[dst_buf.all()],
          dma='const' if eng == 'sp' else 'constp')

    xv = xT.rearrange("(k p) c -> p k c", p=128)
    T_in = [(0, 450), (450, 898), (898, 1346), (1346, 1794), (1794, 2242)]
    def xload(si):
        c0, c1 = T_in[si]
        E('sp', lambda e: e.dma_start(out=X.t[:, :, c0:c1], in_=xv[:, :, c0:c1]),
          reads=([('XORD', si - 2, si - 1)] if si >= 2 else []),
          writes=[X.iv((k,), c0, c1) for k in range(NCH)] + [('XORD', si, si + 1)], dma='x%d' % si)

    xload(0)
    xload(1)
    cload(GA, GA.t[:, :], gall[:, :])
    cload(CW, CW.t[:, :], cwd[:, :])
    cload(CM, CM.t[:, :], cmask_d[:, :])
    cload(RELT, RELT.t[:, :], rel[:, :])
    cload(SNK, SNK.t[0:1, :], sinks[:, :])
    cload(IDB, IDB.t[:, :], ident_d[:, :], 'pool')
    cload(OHB, OHB.t[:, :], onehot[:, :], 'pool')
    cload(SCB, SCB.t[:, :, :, :], scT.rearrange("(k p) b t -> p k b t", p=128), 'pool')
    for si_ in range(2, len(T_in)):
        xload(si_)
    ensure_loaded(NW - 1)

    E('dve', lambda e: e.memset(ONES.t[:, :], 1.0), writes=[ONES.all()])

    def build_consts():
        E('dve', lambda e: e.memset(ESEL.t[:, :, :], 0.0), writes=[ESEL.all()])
        E('dve', lambda e: e.memset(ESEL.t[0:1, 0, 64:128], 1.0), writes=[ESEL.all()])
        E('dve', lambda e: e.memset(ESEL.t[0:1, 1, 0:64], 1.0), writes=[ESEL.all()])
        E('dve', lambda e: e.memset(ESR.t[:, :, :], 0.0), writes=[ESR.all()])
        E('act', lambda e: e.activation(out=ESN.t[0:1, :], in_=SNK.t[0:1, :], func=AF.Exp),
          reads=[SNK.all()], writes=[ESN.all()])
        for v in range(4):
            g, p = v // 2, v % 2
            src = ESN.t[0:1, :].rearrange("o (g hq q) -> o g hq q", g=2, hq=4, q=2)[:, g, :, p]
            E('dve', (lambda vv, s: (lambda e: e.tensor_copy(
                out=ESR.t[0:1, vv * 4:(vv + 1) * 4, :], in_=s.unsqueeze(2).to_broadcast([1, 4, 128]))))(v, src),
              reads=[ESN.all()], writes=[ESR.all()])
        MK['startup_done'] = len(S.ops)

    def bias_prep_dve():
        E('act', lambda e: e.activation(out=EF.t[:, :], in_=RELT.t[:, :], func=AF.Copy, scale=8.0),
          reads=[RELT.all()], writes=[EF.all()])
        E('dve', lambda e: e.tensor_copy(out=EHI.t[:, :], in_=EF.t[:, :]), reads=[EF.all()], writes=[EHI.all()])
        E('dve', lambda e: e.tensor_tensor(out=ELO.t[:, :], in0=EF.t[:, :], in1=EHI.t[:, :], op=ALU.subtract),
          reads=[EF.all(), EHI.all()], writes=[ELO.all()])
        for xi_, EB in enumerate((EHI, ELO)):
            for v in range(4):
                g, p = v // 2, v % 2
                src = EB.t[:, :].rearrange("o (g hq q) -> o g hq q", g=2, hq=4, q=2)[:, g, :, p]
                E('dve', lambda e: e.tensor_copy(
                    out=EBC.t[:, xi_, v * 4:(v + 1) * 4, :], in_=src.unsqueeze(2).to_broadcast([128, 4, 128])),
                  reads=[EB.all()], writes=[EBC.iv((xi_, v * 4), 0, 512)])

    def bias_items():
        for hq in range(16):
            bk = nb()
            mm(ps[:, bk, 0:384], EBC.t[:, 0, hq, :], OHB.t[:, :], True, False,
               [EBC.iv((0, hq), 0, 128), OHB.all()], [psiv(bk, 0, 384)])
            mm(ps[:, bk, 0:384], EBC.t[:, 1, hq, :], OHB.t[:, :], False, True,
               [EBC.iv((1, hq), 0, 128), OHB.all()], [psiv(bk, 0, 384)])
            E('act', lambda e: e.activation(out=REP.t[:, hq % 8, :], in_=ps[:, bk, 0:384], func=AF.Copy),
              reads=[psiv(bk, 0, 384)], writes=[REP.iv((hq % 8,), 0, 384)])
            if hq % 8 == 7:
                hb = hq // 8
                E('sp', lambda e: e.dma_start(out=Bscr.rearrange("h p c -> p h c")[:, hb * 8:(hb + 1) * 8, :],
                                              in_=REP.t[:, :, :]),
                  reads=[REP.all()], writes=[('Bscr', hb, hb + 1)], dma='bscr')
            yield
        MK['bias_done'] = len(S.ops)

    def load_bias_tiles():
        for ti, offv in ((0, 127), (1, 255)):
            src = bass.AP(Bscr_t, offv, [[383, 128], [128 * 384, 16], [1, 128]])
            E('pool', (lambda t_, s: (lambda e: e.dma_start(out=TB.t[:, t_, :, :], in_=s)))(ti, src),
              reads=[('Bscr', 0, 2)], writes=[TB.iv((ti, 0), 0, 2048)], dma='tb%d' % ti)

    def make_tfirst():
        E('dve', lambda e: e.tensor_scalar(out=TB.t[:, 2, :, :], in0=TB.t[:, 1, :, :], scalar1=CM.t[:, 0:1],
                                            scalar2=None, op0=ALU.add),
          reads=[TB.iv((1, 0), 0, 2048), CM.all()], writes=[TB.iv((2, 0), 0, 2048)])


    sq_ctr = [0]
    tmp_ctr = [0]

    def emit_norm(ni, tiles, final=False):
        for ti, (c0, c1) in enumerate(tiles):
            norm_tile(ni, ti, c0, c1, final)

    def norm_tile(ni, ti, c0, c1, final=False):
        if True:
            n = c1 - c0
            bk = nb()
            for k in range(NCH):
                sl = sq_ctr[0] % 5
                sq_ctr[0] += 1
                E('act', (lambda k_, s_: (lambda e: e.activation(out=SQ.t[:, s_, 0:n], in_=X.t[:, k_, c0:c1],
                                                                   func=AF.Square)))(k, sl),
                  reads=[X.iv((k,), c0, c1)], writes=[SQ.iv((sl,), 0, n)])
                mm(ps[:, bk, 0:n], ONES.t[:, :], SQ.t[:, sl, 0:n], k == 0, k == NCH - 1,
                   [ONES.all(), SQ.iv((sl,), 0, n)], [psiv(bk, 0, n)])
            ts = 0
            tmp_ctr[0] += 1
            E('act', lambda e: e.activation(out=TMP.t[:, ts, 0:n], in_=ps[:, bk, 0:n], func=AF.Ln,
                                            scale=1.0 / D, bias=EPS),
              reads=[psiv(bk, 0, n)], writes=[TMP.iv((ts,), 0, n)])
            E('act', lambda e: e.activation(out=ps[:, bk, 0:n], in_=TMP.t[:, ts, 0:n], func=AF.Exp, scale=-0.5),
              reads=[TMP.iv((ts,), 0, n)], writes=[psiv(bk, 0, n)])
            if not final:
                for k in range(NCH):
                    E('dve', (lambda k_: (lambda e: e.scalar_tensor_tensor(
                        out=H.t[:, k_, c0:c1], in0=X.t[:, k_, c0:c1], scalar=GA.t[:, ni * 8 + k_:ni * 8 + k_ + 1],
                        in1=ps[:, bk, 0:n], op0=ALU.mult, op1=ALU.mult)))(k),
                      reads=[X.iv((k,), c0, c1), GA.all(), psiv(bk, 0, n)], writes=[H.iv((k,), c0, c1)])
            else:
                ys = ti % 2
                yv = yT.rearrange("(k p) c -> p k c", p=128)
                for k in range(NCH):
                    E('dve', lambda e: e.scalar_tensor_tensor(
                        out=YST.t[:, ys, k, 0:n], in0=X.t[:, k, c0:c1], scalar=GA.t[:, ni * 8 + k:ni * 8 + k + 1],
                        in1=ps[:, bk, 0:n], op0=ALU.mult, op1=ALU.mult),
                      reads=[X.iv((k,), c0, c1), GA.all(), psiv(bk, 0, n)], writes=[YST.iv((ys, k), 0, n)])
                    E('sp', lambda e: e.dma_start(out=yv[:, k, c0 - C_P0:c1 - C_P0], in_=YST.t[:, ys, k, 0:n]),
                      reads=[YST.iv((ys, k), 0, n)], dma='y%d_%d' % (ys, k))

    norm_tile(0, 0, T_in[0][0], T_in[0][1])
    build_consts()
    bias_prep_dve()
    bgen = bias_items()
    norm_tile(0, 1, T_in[1][0], T_in[1][1])

    MK['norm0_done'] = len(S.ops)
    csb_ctr = [0]
    bsb_ctr = [0]
    acc_ctr = [0]
    pending = [None]

    def conv_step(m, c0, c1, bs):
        z0 = max(c0, 2)
        z1 = c1
        zt1 = min(z1, C_S0)
        nt = zt1 - z0
        ub = m % 2
        bsl = acc_ctr[0] % 3
        acc_ctr[0] += 1
        nz = z1 - z0
        for j in range(3):
            wj = CW.t[:, j * 8 + m:j * 8 + m + 1]
            src = UB.t[:, ub, z0 - 2 + j:zt1 - 2 + j]
            if j == 0:
                E('dve', lambda e: e.tensor_scalar(out=ACC.t[:, bsl, 0:nt], in0=src, scalar1=wj, scalar2=None,
                                                   op0=ALU.mult),
                  reads=[UB.iv((ub,), z0 - 2 + j, zt1 - 2 + j), CW.all()], writes=[ACC.iv((bsl,), 0, nt)])
            else:
                E('dve', lambda e: e.scalar_tensor_tensor(out=ACC.t[:, bsl, 0:nt], in0=src, scalar=wj,
                                                          in1=ACC.t[:, bsl, 0:nt], op0=ALU.mult, op1=ALU.add),
                  reads=[UB.iv((ub,), z0 - 2 + j, zt1 - 2 + j), CW.all(), ACC.iv((bsl,), 0, nt)],
                  writes=[ACC.iv((bsl,), 0, nt)])
        if z1 > C_S0:
            acc = ACC.t[:, bsl, nt:nt + 64].rearrange("p (b t) -> p b t", t=4)
            for j in range(3):
                wj = CW.t[:, j * 8 + m:j * 8 + m + 1]
                src = US.t[:, ub, :, j:j + 4]
                if j == 0:
                    E('dve', lambda e: e.tensor_scalar(out=acc, in0=src, scalar1=wj, scalar2=None, op0=ALU.mult),
                      reads=[US.iv((ub, 0), 0, 96), CW.all()], writes=[ACC.iv((bsl,), nt, nt + 64)])
                else:
                    E('dve', lambda e: e.scalar_tensor_tensor(out=acc, in0=src, scalar=wj, in1=acc,
                                                              op0=ALU.mult, op1=ALU.add),
                      reads=[US.iv((ub, 0), 0, 96), CW.all(), ACC.iv((bsl,), nt, nt + 64)],
                      writes=[ACC.iv((bsl,), nt, nt + 64)])
        E('dve', lambda e: e.tensor_tensor(out=SB.t[:, m, z0:z1], in0=ACC.t[:, bsl, 0:nz],
                                           in1=BSB.t[:, bs, z0 - c0:z1 - c0], op=ALU.mult),
          reads=[ACC.iv((bsl,), 0, nz), BSB.iv((bs,), z0 - c0, z1 - c0)], writes=[SB.iv((m,), z0, z1)])

    for mgrp in ((0, 1), (2,), (3,), (4,), (5,), (6,), (7,)):
        sls = {}
        for m in mgrp:
            sls[m] = (use_piece(), use_piece(), use_piece())
            E('dve', lambda e: e.tensor_copy(out=US.t[:, m % 2, :, 0:2], in_=SCB.t[:, m, :, :]),
              reads=[SCB.all()], writes=[US.iv((m % 2, 0), 0, 96)])
        for si, (c0, c1) in enumerate(T_in):
            if mgrp == (2,):
                for _ in range(3):
                    next(bgen, None)
            if mgrp == (3,) and si == 0:
                next(bgen, None)
            if mgrp == (0, 1) and si + 2 < len(T_in):
                norm_tile(0, si + 2, T_in[si + 2][0], T_in[si + 2][1])
            for m in mgrp:
                ub = m % 2
                n = c1 - c0
                bc, bx, bb = nb(), nb(), nb()
                for (bk, sl) in ((bc, sls[m][0]), (bx, sls[m][1]), (bb, sls[m][2])):
                    for k in range(NCH):
                        mm(ps[:, bk, 0:n], WR.t[:, sl, k, :], H.t[:, k, c0:c1], k == 0, k == NCH - 1,
                           [WR.iv((sl, k), 0, 128), H.iv((k,), c0, c1)], [psiv(bk, 0, n)])
                cs = csb_ctr[0] % 2
                csb_ctr[0] += 1
                E('act', lambda e: e.activation(out=CSB.t[:, cs, 0:n], in_=ps[:, bc, 0:n], func=AF.Copy),
                  reads=[psiv(bc, 0, n)], writes=[CSB.iv((cs,), 0, n)])
                t1 = min(c1, C_S0)
                E('dve', lambda e: e.tensor_tensor(out=UB.t[:, ub, c0:t1], in0=CSB.t[:, cs, 0:t1 - c0],
                                                   in1=ps[:, bx, 0:t1 - c0], op=ALU.mult),
                  reads=[CSB.iv((cs,), 0, t1 - c0), psiv(bx, 0, t1 - c0)], writes=[UB.iv((ub,), c0, t1)])
                if c1 > C_S0:
                    o0 = C_S0 - c0
                    E('dve', lambda e: e.tensor_tensor(
                        out=US.t[:, ub, :, 2:6], in0=CSB.t[:, cs, o0:o0 + 64].rearrange("p (b t) -> p b t", t=4),
                        in1=ps[:, bx, o0:o0 + 64].rearrange("p (b t) -> p b t", t=4), op=ALU.mult),
                      reads=[CSB.iv((cs,), o0, o0 + 64), psiv(bx, o0, o0 + 64)], writes=[US.iv((ub, 0), 0, 96)])
                    o1 = C_S0 - 2 - c0
                    E('dve', lambda e: e.tensor_tensor(out=U32.t[:, m, :], in0=CSB.t[:, cs, o1:o1 + 66],
                                                       in1=ps[:, bx, o1:o1 + 66], op=ALU.mult),
                      reads=[CSB.iv((cs,), o1, o1 + 66), psiv(bx, o1, o1 + 66)], writes=[U32.iv((m,), 0, 66)])
                bs = bsb_ctr[0] % 3
                bsb_ctr[0] += 1
                E('act', lambda e: e.activation(out=BSB.t[:, bs, 0:n], in_=ps[:, bb, 0:n], func=AF.Copy),
                  reads=[psiv(bb, 0, n)], writes=[BSB.iv((bs,), 0, n)])
                if pending[0] is not None:
                    conv_step(*pending[0])
                pending[0] = (m, c0, c1, bs)
        release_all()
    conv_step(*pending[0])
    pending[0] = None
    E('sp', lambda e: e.dma_start(out=cks[:, 0:124, :], in_=ck[:, 4:128, :]), dma='outc')
    E('sp', lambda e: e.dma_start(out=cvs[:, 0:124, :], in_=cv[:, 4:128, :]), dma='outc')
    for _ in bgen:
        pass
    load_bias_tiles()
    uv = uT.rearrange("(k p) c -> p k c", p=128)
    E('sp', lambda e: e.dma_start(out=uv[:, :, :], in_=U32.t[:, :, :]), reads=[U32.all()], dma='outc')

    MK['convin_done'] = len(S.ops)
    A_x = split(C_H0, NT)
    A_q = split(C_P0, NT)

    deferred = []

    def run_deferred():
        while deferred:
            deferred.pop(0)()

    def proj_add(src_buf, tiles, next_norm):
        n0 = piece_ctr[0]
        slots = [use_piece() for _ in range(NCH)]
        ni, fin = next_norm
        for si, (c0, c1) in enumerate(tiles):
            n = c1 - c0
            for m in range(NCH):
                sl = slots[m]
                bk = nb()
                for k in range(NCH):
                    mm(ps[:, bk, 0:n], WR.t[:, sl, k, :], src_buf.t[:, k, c0:c1], k == 0, k == NCH - 1,
                       [WR.iv((sl, k), 0, 128), src_buf.iv((k,), c0, c1)], [psiv(bk, 0, n)])
                E('dve', lambda e: e.tensor_tensor(out=X.t[:, m, c0:c1], in0=X.t[:, m, c0:c1],
                                                   in1=ps[:, bk, 0:n], op=ALU.add),
                  reads=[X.iv((m,), c0, c1), psiv(bk, 0, n)], writes=[X.iv((m,), c0, c1)])
                if si == len(tiles) - 1:
                    release_upto(n0 + m)
            if si >= 1:
                norm_tile(ni, si - 1, tiles[si - 1][0], tiles[si - 1][1], fin)
        if fin:
            norm_tile(ni, len(tiles) - 1, tiles[-1][0], tiles[-1][1], fin)
        else:
            deferred.append(lambda: norm_tile(ni, len(tiles) - 1, tiles[-1][0], tiles[-1][1], fin))

    proj_add(SB, A_x, (1, False))

    MK['convout_done'] = len(S.ops)
    sg_ctr = [0]

    def emit_ffn(tiles, next_norm):
        def gu_item(jj, slg, slu, c0, c1):
            n = c1 - c0
            bg, bu = nb(), nb()
            for (bk, sl) in ((bg, slg), (bu, slu)):
                for k in range(NCH):
                    mm(ps[:, bk, 0:n], WR.t[:, sl, k, :], H.t[:, k, c0:c1], k == 0, k == NCH - 1,
                       [WR.iv((sl, k), 0, 128), H.iv((k,), c0, c1)], [psiv(bk, 0, n)])
            ss = sg_ctr[0] % 3
            sg_ctr[0] += 1
            E('act', lambda e: e.activation(out=SG.t[:, ss, 0:n], in_=ps[:, bg, 0:n], func=AF.Silu),
              reads=[psiv(bg, 0, n)], writes=[SG.iv((ss,), 0, n)])
            E('dve', lambda e: e.tensor_tensor(out=SB.t[:, jj, c0:c1], in0=SG.t[:, ss, 0:n],
                                               in1=ps[:, bu, 0:n], op=ALU.mult),
              reads=[SG.iv((ss,), 0, n), psiv(bu, 0, n)], writes=[SB.iv((jj,), c0, c1)])

        for grp in FF_GROUPS:
            jj0 = 0
            if grp is FF_GROUPS[0] and deferred:
                sl = [(use_piece(), use_piece()) for _ in range(2)]
                T = tiles
                gu_item(0, sl[0][0], sl[0][1], *T[0])
                gu_item(0, sl[0][0], sl[0][1], *T[1])
                run_deferred()
                for t in T[2:-1]:
                    gu_item(0, sl[0][0], sl[0][1], *t)
                for t in T[:-1]:
                    gu_item(1, sl[1][0], sl[1][1], *t)
                gu_item(0, sl[0][0], sl[0][1], *T[-1])
                gu_item(1, sl[1][0], sl[1][1], *T[-1])
                release_all()
                jj0 = 2
            for jj in range(jj0, len(grp)):
                sl_g = use_piece()
                sl_u = use_piece()
                for (c0, c1) in tiles:
                    gu_item(jj, sl_g, sl_u, c0, c1)
                release_all()
            ng = len(grp)
            last = grp is FF_GROUPS[-1]
            if not last:
                for m in range(NCH):
                    sl = use_piece()
                    for (c0, c1) in tiles:
                        n = c1 - c0
                        bk = nb()
                        for jj in range(ng):
                            mm(ps[:, bk, 0:n], WR.t[:, sl, jj, :], SB.t[:, jj, c0:c1], jj == 0, jj == ng - 1,
                               [WR.iv((sl, jj), 0, 128), SB.iv((jj,), c0, c1)], [psiv(bk, 0, n)])
                        E('dve', lambda e: e.tensor_tensor(out=X.t[:, m, c0:c1], in0=X.t[:, m, c0:c1],
                                                           in1=ps[:, bk, 0:n], op=ALU.add),
                          reads=[X.iv((m,), c0, c1), psiv(bk, 0, n)], writes=[X.iv((m,), c0, c1)])
                    release_all()
            else:
                n0 = piece_ctr[0]
                slots = [use_piece() for _ in range(NCH)]
                ni, fin = next_norm
                if fin:
                    lc0, lc1 = tiles[-1]
                    tiles = tiles[:-1] + [(lc0, lc1 - 120), (lc1 - 120, lc1)]
                for si, (c0, c1) in enumerate(tiles):
                    n = c1 - c0
                    for m in range(NCH):
                        sl = slots[m]
                        bk = nb()
                        for jj in range(ng):
                            mm(ps[:, bk, 0:n], WR.t[:, sl, jj, :], SB.t[:, jj, c0:c1], jj == 0, jj == ng - 1,
                               [WR.iv((sl, jj), 0, 128), SB.iv((jj,), c0, c1)], [psiv(bk, 0, n)])
                        E('dve', lambda e: e.tensor_tensor(out=X.t[:, m, c0:c1], in0=X.t[:, m, c0:c1],
                                                           in1=ps[:, bk, 0:n], op=ALU.add),
                          reads=[X.iv((m,), c0, c1), psiv(bk, 0, n)], writes=[X.iv((m,), c0, c1)])
                        if si == len(tiles) - 1:
                            release_upto(n0 + m)
                    if si >= 1:
                        norm_tile(ni, si - 1, tiles[si - 1][0], tiles[si - 1][1], fin)
                if fin:
                    norm_tile(ni, len(tiles) - 1, tiles[-1][0], tiles[-1][1], fin)
                else:
                    deferred.append(lambda: norm_tile(ni, len(tiles) - 1, tiles[-1][0], tiles[-1][1], fin))

    emit_ffn(A_x, (2, False))

    MK['ffn0_done'] = len(S.ops)
    def kp_item(v, sl, c0, c1):
        n = c1 - c0
        bk = nb()
        for k in range(NCH):
            mm(ps[:, bk, 0:n], WR.t[:, sl, k, :], H.t[:, k, c0:c1], k == 0, k == NCH - 1,
               [WR.iv((sl, k), 0, 128), H.iv((k,), c0, c1)], [psiv(bk, 0, n)])
        E('act', lambda e: e.activation(out=KP.t[:, v, c0:c1], in_=ps[:, bk, 0:n], func=AF.Copy),
          reads=[psiv(bk, 0, n)], writes=[KP.iv((v,), c0, c1)])

    slk = [use_piece() for _ in range(4)]
    kp_item(0, slk[0], *A_x[0])
    kp_item(0, slk[0], *A_x[1])
    kp_item(1, slk[1], *A_x[0])
    kp_item(1, slk[1], *A_x[1])
    run_deferred()
    for v in range(4):
        for t in (A_x[2:-1] if v < 2 else A_x[:-1]):
            kp_item(v, slk[v], *t)
    for v in range(4):
        kp_item(v, slk[v], *A_x[-1])
    release_all()
    MK['kpad_done'] = len(S.ops)
    E('dve', lambda e: e.memset(VA.t[:, :, :], 1.0), writes=[VA.all()])
    blocks = [(C_H0 + 128 * i, C_H0 + 128 * (i + 1)) for i in range(17)] + [(NT - 128, NT)]
    sl_v = use_piece()
    for bi, (b0, b1) in enumerate(blocks):
        bk = nb()
        for k in range(NCH):
            mm(ps[:, bk, 0:128], H.t[:, k, b0:b1], WR.t[:, sl_v, k, :], k == 0, k == NCH - 1,
               [WR.iv((sl_v, k), 0, 128), H.iv((k,), b0, b1)], [psiv(bk, 0, 128)])
        E('dve', (lambda bi_: (lambda e: e.tensor_copy(
            out=VA.t[:, bi_, 64:320].rearrange("p (a c) -> p a c", c=128)[:, :, 0:64],
            in_=ps[:, bk, 0:128].rearrange("p (a c) -> p a c", c=64))))(bi),
          reads=[psiv(bk, 0, 128)], writes=[VA.iv((bi,), 0, 320)])
        if bi >= 16:
            vs = bi - 16
            E('dve', lambda e: e.tensor_copy(out=VST.t[:, vs, :], in_=ps[:, bk, 0:128]),
              reads=[psiv(bk, 0, 128)], writes=[VST.iv((vs,), 0, 128)])
            if bi == 16:
                E('sp', lambda e: e.dma_start(out=kvp[1, :, :], in_=VST.t[:, 0, :]),
                  reads=[VST.iv((0,), 0, 128)], dma='outc')
            else:
                for b in range(16):
                    E('sp', (lambda b_: (lambda e: e.dma_start(out=cvs[b_, 124:128, :],
                                                                in_=VST.t[64 + 4 * b_:68 + 4 * b_, 1, :])))(b),
                      reads=[VST.iv((1,), 0, 128)], dma='outc')
    MK['vtok_done'] = len(S.ops)
    release_all()
    sl_k = use_piece()
    for bi in (16, 17):
        b0, b1 = blocks[bi]
        bk = nb()
        for k in range(NCH):
            mm(ps[:, bk, 0:128], H.t[:, k, b0:b1], WR.t[:, sl_k, k, :], k == 0, k == NCH - 1,
               [WR.iv((sl_k, k), 0, 128), H.iv((k,), b0, b1)], [psiv(bk, 0, 128)])
        ks = bi - 16
        E('act', lambda e: e.activation(out=KST.t[:, ks, :], in_=ps[:, bk, 0:128], func=AF.Copy),
          reads=[psiv(bk, 0, 128)], writes=[KST.iv((ks,), 0, 128)])
        if bi == 16:
            E('sp', lambda e: e.dma_start(out=kvp[0, :, :], in_=KST.t[:, 0, :]),
              reads=[KST.iv((0,), 0, 128)], dma='outc')
        else:
            for b in range(16):
                E('sp', (lambda b_: (lambda e: e.dma_start(out=cks[b_, 124:128, :],
                                                            in_=KST.t[64 + 4 * b_:68 + 4 * b_, 1, :])))(b),
                  reads=[KST.iv((1,), 0, 128)], dma='outc')
    MK['ktok_done'] = len(S.ops)
    release_all()
    for m in range(NCH):
        sl = use_piece()
        for (c0, c1) in A_q:
            n = c1 - c0
            bk = nb()
            for k in range(NCH):
                mm(ps[:, bk, 0:n], WR.t[:, sl, k, :], H.t[:, k, c0:c1], k == 0, k == NCH - 1,
                   [WR.iv((sl, k), 0, 128), H.iv((k,), c0, c1)], [psiv(bk, 0, n)])
            E('act', (lambda m_: (lambda e: e.activation(out=SB.t[:, m_, c0:c1], in_=ps[:, bk, 0:n], func=AF.Copy)))(m),
              reads=[psiv(bk, 0, n)], writes=[SB.iv((m,), c0, c1)])
        release_all()

    MK['qkv_done'] = len(S.ops)
    VSL = {0: (64, 192), 1: (0, 128), 2: (192, 320), 3: (128, 256)}
    def sample_loads():
        ckv = ck.rearrange("b k c -> k b c")
        E('pool', lambda e: e.dma_start(out=KC2.t[:, :, 0, :], in_=ckv), writes=[KC2.all()], dma='kc')
        E('pool', lambda e: e.dma_start(out=KC2.t[:, :, 1, 0:64], in_=ckv[:, :, 64:128]), writes=[KC2.all()], dma='kc')
        E('pool', lambda e: e.dma_start(out=KC2.t[:, :, 1, 64:128], in_=ckv[:, :, 0:64]), writes=[KC2.all()], dma='kc')
        E('pool', lambda e: e.memset(VAS.t[:, :, :], 1.0), writes=[VAS.all()])
        cvv = cv.rearrange("b k (a c) -> k b a c", c=64)
        for a in range(2):
            E('pool', lambda e: e.dma_start(out=VAS.t[:, :, 64 + 128 * a:128 + 128 * a], in_=cvv[:, :, a, :]),
              writes=[VAS.all()], dma='vc')
        E('pool', lambda e: e.memset(KCT.t[:, :, :, :], 0.0), writes=[KCT.all(), ('KCTb', 0, 4)])
        E('pool', lambda e: e.memset(OWN.t[:, :, :, :, :], -240000.0), writes=[OWN.all(), ('OWNd', 0, 64)])
        for b in range(16):
            for v in range(4):
                src = bass.AP(Bscr_t, 127 + v * 4 * 128 * 384, [[383, 4], [128 * 384, 4], [1, 4]])
                E('sp', lambda e: e.dma_start(out=OWN.t[64 + 4 * b:68 + 4 * b, v, b, :, :], in_=src),
                  reads=[('Bscr', 0, 2)], writes=[('OWNd', b * 4 + v, b * 4 + v + 1)], dma='own')

    units = [(i, v) for i in range(1, 17) for v in range(4)]
    NU = len(units)

    def qk_unit(idx):
        i, v = units[idx]
        g = v // 2
        q0, q1 = blocks[i]
        sb0 = (2 * idx) % 4
        pb0 = (2 * idx) % 4
        for kb in range(2):
            sb_ = sb0 + kb
            k0, k1 = blocks[i - 1 + kb]
            ti = (2 if i == 1 else 1) if kb == 0 else 0
            mm(ps[:, sb_, 0:512], IDB.t[:, :], TB.t[:, ti, 4 * v:4 * v + 4, :], True, False,
               [IDB.all(), TB.iv((ti, 4 * v), 0, 512)], [psiv(sb_, 0, 512)])
            mm(ps[:, sb_, 0:512], KP.t[:, v, k0:k1],
               SB.t[:, 4 * g:4 * g + 4, q0:q1], False, True,
               [KP.iv((v,), k0, k1)] + [SB.iv((c,), q0, q1) for c in range(4 * g, 4 * g + 4)],
               [psiv(sb_, 0, 512)])
        E('act', lambda e: e.activation(out=PB.t[:, pb0:pb0 + 2, :], in_=ps[:, sb0:sb0 + 2, 0:512], func=AF.Exp, scale=0.125),
          reads=[psiv(sb0, 0, 512), psiv(sb0 + 1, 0, 512)], writes=[PB.iv((pb0,), 0, 1024)])

    def pv_unit(idx):
        i, v = units[idx]
        g, p = v // 2, v % 2
        q0, q1 = blocks[i]
        ob = 4 + (idx % 2)
        a0, a1 = VSL[v]
        for kb in range(2):
            sb_ = (2 * idx) % 4 + kb
            mm(ps[:, ob, 0:512], VA.t[:, i - 1 + kb, a0:a1], PB.t[:, sb_, :], kb == 0, False,
               [VA.iv((i - 1 + kb,), 0, 320), PB.iv((sb_,), 0, 512)], [psiv(ob, 0, 512)])
        mm(ps[:, ob, 0:512], ESEL.t[:, p, :], ESR.t[:, 4 * v:4 * v + 4, :],
           False, True, [ESEL.all(), ESR.all()], [psiv(ob, 0, 512)])
        vh = slice(0, 64) if p == 0 else slice(64, 128)
        dh = slice(64, 128) if p == 0 else slice(0, 64)
        rs = idx % 2
        CS = 416
        E('act', lambda e: e.activation(out=REC.t[vh, rs, 0:CS], in_=ps[dh, ob, 0:CS], func=AF.Ln),
          reads=[psiv(ob, 0, 512)], writes=[REC.iv((rs,), 0, CS), ('LNord', rs, rs + 1)])
        E('dve', lambda e: e.reciprocal(out=REC.t[vh, rs, CS:512], in_=ps[dh, ob, CS:512]),
          reads=[psiv(ob, 0, 512), ('LNord', rs, rs + 1)], writes=[REC.iv((rs,), CS, 512)])
        E('act', lambda e: e.activation(out=REC.t[vh, rs, 0:CS], in_=REC.t[vh, rs, 0:CS], func=AF.Exp, scale=-1.0),
          reads=[REC.iv((rs,), 0, CS)], writes=[REC.iv((rs,), 0, CS)])
        E('dve', lambda e: e.tensor_tensor(
            out=SB.t[vh, 4 * g:4 * g + 4, q0:q1], in0=ps[vh, ob, 0:512].rearrange("p (h q) -> p h q", q=128),
            in1=REC.t[vh, rs, :].rearrange("p (h q) -> p h q", q=128), op=ALU.mult),
          reads=[psiv(ob, 0, 512), REC.iv((rs,), 0, 512)],
          writes=[SB.iv((c,), q0, q1) for c in range(4 * g, 4 * g + 4)])

    SC0 = C_S0
    w0, w1 = blocks[17]
    BX, BS = 6, 7
    PPs = (PP0, PP1)

    def sample_steps():
        def kt_step(b):
            ks = b % 4
            for a, var in enumerate((0, 1, 0)):
                mm(ps[:, BX, a * 128:(a + 1) * 128], KC2.t[:, b, var, :], IDB.t[:, :], True, True,
                   [KC2.iv((b, var), 0, 128), IDB.all()], [psiv(BX, 0, 512)])
            E('dve', lambda e: e.tensor_copy(
                out=KCT.t[0:64, ks, :, :].rearrange("p (a two) c -> p a two c", two=2)[:, :, 0, :],
                in_=ps[0:64, BX, 0:256].rearrange("p (a c) -> p a c", c=128)),
              reads=[psiv(BX, 0, 256)], writes=[KCT.iv((ks, 0), 0, 512)])
            E('dve', lambda e: e.tensor_copy(
                out=KCT.t[64:128, ks, :, :].rearrange("p (a two) c -> p a two c", two=2)[:, :, 1, :],
                in_=ps[64:128, BX, 128:384].rearrange("p (a c) -> p a c", c=128)),
              reads=[psiv(BX, 128, 384)], writes=[KCT.iv((ks, 0), 0, 512)])

        def sprev_step(b):
            ks = b % 4
            bl = b % 8
            for v in range(4):
                g = v // 2
                cc = bl * 64 + v * 16
                mm(ps[:, BS, cc:cc + 16], KCT.t[:, ks, v, :],
                   SB.t[:, 4 * g:4 * g + 4, SC0 + 4 * b:SC0 + 4 * b + 4], False, (bl == 7 and v == 3),
                   [KCT.iv((ks, v), 0, 128)] + [SB.iv((c,), SC0 + 4 * b, SC0 + 4 * b + 4) for c in range(4 * g, 4 * g + 4)],
                   [psiv(BS, cc, cc + 16)])

        kt_step(0)
        yield
        for half in range(2):
            mm(ps[:, BS, 0:512], IDB.t[:, :], TB.t[:, 1, :, 0:4].unsqueeze(1).to_broadcast([128, 8, 16, 4]), True, False,
               [IDB.all(), TB.iv((1, 0), 0, 2048)], [psiv(BS, 0, 512)])
            for bl in range(8):
                b = half * 8 + bl
                if b + 1 < 16:
                    kt_step(b + 1)
                sprev_step(b)
                yield
            PPh = PPs[half]
            ppv = PPh.t[:, :].rearrange("p (b h t) -> p b h t", b=8, t=5)[:, :, :, 0:4]
            E('act', lambda e: e.activation(out=ppv, in_=ps[:, BS, 0:512].rearrange("p (b h t) -> p b h t", b=8, t=4),
                                            func=AF.Exp, scale=0.125),
              reads=[psiv(BS, 0, 512)], writes=[PPh.all()])
            yield
        E('dve', lambda e: e.tensor_copy(out=OWNB.t[:, :], in_=OWN.t[:, :, :, :, :].rearrange("p v b h t -> p (v b h t)")),
          reads=[OWN.all(), ('OWNd', 0, 64)], writes=[OWNB.all()])
        yield
        for hb in range(2):
            mm(ps[:, BX, 0:512], IDB.t[:, :], OWNB.t[:, hb * 512:(hb + 1) * 512], True, False,
               [IDB.all(), OWNB.all()], [psiv(BX, 0, 512)])
            for v in (2 * hb, 2 * hb + 1):
                g = v // 2
                cc = (v % 2) * 256
                mm(ps[:, BX, cc:cc + 256], KP.t[:, v, w0:w1],
                   SB.t[:, 4 * g:4 * g + 4, SC0:SC0 + 64].rearrange("p h (b t) -> p b h t", t=4), False, v % 2 == 1,
                   [KP.iv((v,), w0, w1)] + [SB.iv((c,), SC0, SC0 + 64) for c in range(4 * g, 4 * g + 4)],
                   [psiv(BX, cc, cc + 256)])
            E('act', lambda e: e.activation(out=PO.t[:, :], in_=ps[:, BX, 0:512], func=AF.Exp, scale=0.125),
              reads=[psiv(BX, 0, 512)], writes=[PO.all()])
            yield
            for v in (2 * hb, 2 * hb + 1):
                g, p = v // 2, v % 2
                cc = (v % 2) * 256
                a0, a1 = VSL[v]
                mm(ps[:, BS, cc:cc + 256], VA.t[:, 17, a0:a1], PO.t[:, cc:cc + 256], True, False,
                   [VA.iv((17,), 0, 320), PO.iv((), cc, cc + 256)], [psiv(BS, cc, cc + 256)])
                for b in range(16):
                    half, bl = b // 8, b % 8
                    pc = bl * 80 + v * 20
                    mm(ps[:, BS, cc + 16 * b:cc + 16 * b + 16],
                       VAS.t[:, b, a0:a1], PPs[half].t[:, pc:pc + 20].rearrange("p (h t) -> p h t", t=5)[:, :, 0:4],
                       False, False,
                       [VAS.iv((b,), 0, 320), PPs[half].iv((), pc, pc + 20)], [psiv(BS, cc, cc + 256)])
                mm(ps[:, BS, cc:cc + 256], ESEL.t[:, p, :],
                   ESR.t[:, 4 * v:4 * v + 4, 0:4].unsqueeze(1).to_broadcast([128, 16, 4, 4]), False, True,
                   [ESEL.all(), ESR.all()], [psiv(BS, cc, cc + 256)])
                vh = slice(0, 64) if p == 0 else slice(64, 128)
                dh = slice(64, 128) if p == 0 else slice(0, 64)
                E('act', lambda e: e.activation(out=RCS.t[vh, 0, :], in_=ps[dh, BS, cc:cc + 256], func=AF.Ln),
                  reads=[psiv(BS, cc, cc + 256)], writes=[RCS.iv((0,), 0, 256)])
                E('act', lambda e: e.activation(out=RCS.t[vh, 0, :], in_=RCS.t[vh, 0, :], func=AF.Exp, scale=-1.0),
                  reads=[RCS.iv((0,), 0, 256)], writes=[RCS.iv((0,), 0, 256)])
                E('dve', lambda e: e.tensor_tensor(
                    out=SB.t[vh, 4 * g:4 * g + 4, SC0:SC0 + 64].rearrange("p h (b t) -> p b h t", t=4),
                    in0=ps[vh, BS, cc:cc + 256].rearrange("p (b h t) -> p b h t", h=4, t=4),
                    in1=RCS.t[vh, 0, :].rearrange("p (b h t) -> p b h t", h=4, t=4), op=ALU.mult),
                  reads=[psiv(BS, cc, cc + 256), RCS.iv((0,), 0, 256)],
                  writes=[SB.iv((c,), SC0, SC0 + 64) for c in range(4 * g, 4 * g + 4)])
                yield

    make_tfirst()
    sample_loads()
    sgen = sample_steps()
    for idx in range(NU + 1):
        if idx < NU:
            qk_unit(idx)
        if idx >= 1:
            pv_unit(idx - 1)
        if idx >= 13 and idx % 2 == 1:
            next(sgen, None)
    for _ in sgen:
        pass

    MK['pattn_done'] = len(S.ops)
    MK['sattn_done'] = len(S.ops)
    proj_add(SB, A_q, (3, False))
    emit_ffn(A_q, (4, True))

    assert piece_ctr[0] == len(pieces), (piece_ctr[0], len(pieces))
    assert not deferred

    if marks is not None:
        marks.update(MK)
    if max_ops is not None:
        del S.ops[max_ops:]
    final_groups = ['outc'] + ['y%d_%d' % (a, k) for a in range(2) for k in range(NCH)]
    sem_keys = S.finalize(nc, final_groups)
    sems = {}
    for key in sem_keys:
        sems[key] = nc.alloc_semaphore(name="s_%s_%s" % key)
    with nc.Block() as block:
        @block.tensor
        def _(e):
            S.run(nc, sems, final_groups, ('const', 'constp', 'own'))('pe', e)

        @block.scalar
        def _(e):
            S.run(nc, sems, final_groups, ('const', 'constp', 'own'))('act', e)

        @block.vector
        def _(e):
            S.run(nc, sems, final_groups, ('const', 'constp', 'own'))('dve', e)

        @block.gpsimd
        def _(e):
            S.run(nc, sems, final_groups, ('const', 'constp', 'own'))('pool', e)

        @block.sync
        def _(e):
            S.run(nc, sems, final_groups, ('const', 'constp', 'own'))('sp', e)
    return nc


_PROG = {}


def _get_prog():
    if 'nc' not in _PROG:
        _PROG['nc'] = build_program()
    return _PROG['nc']


def make_in_maps(inputs):
    f = lambda a: np.ascontiguousarray(np.asarray(a, dtype=np.float32))
    x_prompt = f(inputs['x_prompt'])[0]
    x_sample = f(inputs['x_sample'])
    state_conv = f(inputs['state_conv'])[0]
    cache_k = f(inputs['cache_k'])[0].reshape(128, 128, 128)
    cache_v = f(inputs['cache_v'])[0].reshape(128, 128, 128)
    g_mix, g_ffn, g_final = f(inputs['g_mix']), f(inputs['g_ffn']), f(inputs['g_final'])
    gs = np.stack([g_mix[0], g_ffn[0], g_mix[1], g_ffn[1], g_final], 0)
    gall = np.ascontiguousarray(gs.reshape(5, 8, 128).transpose(2, 0, 1).reshape(128, 40))
    cw = f(inputs['conv_w'])[0]
    cwd = np.ascontiguousarray(cw.reshape(3, 8, 128).transpose(2, 0, 1).reshape(128, 24))
    rel = np.zeros((128, 16), np.float32)
    rel[:32] = f(inputs['rel_table'])
    rel[32] = -30000.0
    sinks = f(inputs['sinks']).reshape(1, 16)
    onehot = np.zeros((128, 384), np.float32)
    for i in range(383):
        dist = i - 127
        if 0 <= dist <= 128:
            onehot[int(t5_bucket_np(np.array(dist, np.int32))), i] = 1.0
        else:
            onehot[32, i] = 1.0
    ident = np.eye(128, dtype=np.float32)
    w_k = f(inputs['w_k'])[0]
    wkpad = np.zeros((4, D, 128), np.float32)
    for g in range(2):
        for p in range(2):
            wkpad[g * 2 + p][:, p * 64:(p + 1) * 64] = w_k[:, g * 64:(g + 1) * 64]
    shared = dict(
        gall=gall, cwd=cwd, sinks=sinks, rel=rel, onehot=onehot, ident=ident,
        w_conv_in=f(inputs['w_conv_in'])[0], w_conv_out=f(inputs['w_conv_out'])[0],
        w_q=f(inputs['w_q'])[0], wkpad=wkpad, w_k=w_k, w_v=f(inputs['w_v'])[0], w_o=f(inputs['w_o'])[0],
        w_gate=f(inputs['w_gate']), w_up=f(inputs['w_up']), w_down=f(inputs['w_down']),
    )
    xpad = np.concatenate([np.zeros((130, D), np.float32), x_prompt], 0)
    in_maps = []
    for c in range(NCORES):
        rows = np.concatenate([
            xpad[2048 * c:2048 * c + 2],
            xpad[2048 * c + 2:2048 * c + 130],
            x_prompt[2048 * c:2048 * (c + 1)],
            x_sample[16 * c:16 * (c + 1)].reshape(64, D),
        ], 0)
        m = dict(shared)
        m['xT'] = np.ascontiguousarray(rows.T)
        m['scT'] = np.ascontiguousarray(state_conv[16 * c:16 * (c + 1)].transpose(2, 0, 1))
        m['ck'] = np.ascontiguousarray(cache_k[16 * c:16 * (c + 1)])
        m['cv'] = np.ascontiguousarray(cache_v[16 * c:16 * (c + 1)])
        m['cmask'] = np.full((128, 1), -240000.0 if c == 0 else 0.0, np.float32)
        in_maps.append(m)
    return in_maps


def assemble(results):
    yp = np.concatenate([r['yT'][:, :2048].T for r in results], 0)[None]
    ys = np.concatenate([r['yT'][:, 2048:].T.reshape(16, 4, D) for r in results], 0)
    scp = np.ascontiguousarray(results[-1]['uT'][:, 0:2].T)[None, None]
    scs = np.concatenate([r['uT'][:, 2:66].T.reshape(16, 4, D)[:, 2:4] for r in results], 0)[None]
    ckp = results[-1]['kvp'][0].reshape(1, 1, 128, 2, 64)
    cvp = results[-1]['kvp'][1].reshape(1, 1, 128, 2, 64)
    cks = np.concatenate([r['cks'] for r in results], 0).reshape(1, 128, 128, 2, 64)
    cvs = np.concatenate([r['cvs'] for r in results], 0).reshape(1, 128, 128, 2, 64)
    out = (yp, ys, scp, scs, ckp, cks, cvp, cvs)
    return tuple(np.ascontiguousarray(o, dtype=np.float32) for o in out)


def kernel(**inputs):
    nc = _get_prog()
    in_maps = make_in_maps(inputs)
    res = run_bass_kernel_spmd(nc, in_maps, core_ids=list(range(NCORES)))
    return assemble(res.results)
```

```python
import math
import numpy as np
import concourse.bass as bass
import concourse.mybir as mybir
from concourse.bass_utils import run_bass_kernel_spmd

F32 = mybir.dt.float32
BF16 = mybir.dt.bfloat16
AF = mybir.ActivationFunctionType
ALU = mybir.AluOpType

NCORES = 8
D = 1024
NCH = 8
DFF = 2816
NFF = 22
NT = 2242
C_H0 = 2
C_P0 = 130
C_S0 = 2178
NQ = NT - C_P0
EPS = 1e-5
FF_GROUPS = [list(range(0, 8)), list(range(8, 15)), list(range(15, 22))]
NW = 8


def split(c0, c1, n=5):
    tot = c1 - c0
    base = tot // n
    rem = tot % n
    out = []
    c = c0
    for i in range(n):
        w = base + (1 if i < rem else 0)
        out.append((c, c + w))
        c += w
    return out


class Sched:
    def __init__(self):
        self.ops = []
        self.regions = {}

    def _recs(self, rg):
        return self.regions.setdefault(rg, {})

    def emit(self, eng, fn, reads=(), writes=(), dma=None):
        op = dict(id=len(self.ops), eng=eng, fn=fn, deps=set(), dma=dma, signal=False)
        for (rg, b0, b1) in reads:
            recs = self._recs(rg)
            for (k0, k1), rec in recs.items():
                if k0 < b1 and b0 < k1 and rec[0] is not None:
                    op['deps'].add(rec[0])
            rec = recs.get((b0, b1))
            if rec is None:
                rec = [None, {}]
                recs[(b0, b1)] = rec
            rec[1][eng if dma is None else ('dma', op['id'])] = op['id']
        for (rg, b0, b1) in writes:
            recs = self._recs(rg)
            dele = []
            for (k0, k1), rec in recs.items():
                if k0 < b1 and b0 < k1:
                    if rec[0] is not None:
                        op['deps'].add(rec[0])
                    for r in rec[1].values():
                        op['deps'].add(r)
                    if b0 <= k0 and k1 <= b1:
                        dele.append((k0, k1))
            for k in dele:
                del recs[k]
            recs[(b0, b1)] = [op['id'], {}]
        op['deps'].discard(op['id'])
        self.ops.append(op)
        return op

    def finalize(self, nc, final_groups):
        ops = self.ops
        engs = ['pe', 'act', 'dve', 'pool', 'sp']
        for op in ops:
            for d in op['deps']:
                dop = ops[d]
                if dop['dma'] is not None:
                    continue
                if dop['eng'] == 'pe' and op['eng'] == 'pe' and op['dma'] is None:
                    continue
                dop['signal'] = True
        cnt = {e: 0 for e in engs}
        gcnt = {}
        for op in ops:
            if op['dma'] is not None:
                g = op['dma']
                gcnt[g] = gcnt.get(g, 0) + 16
                op['sig'] = (('g', g), gcnt[g])
            elif op['signal']:
                cnt[op['eng']] += 1
                op['sig'] = (('e', op['eng']), cnt[op['eng']])
        self.gtotal = dict(gcnt)
        sem_keys = [('e', e) for e in engs] + [('g', g) for g in gcnt]
        return sem_keys

    def run(self, nc, sems, final_groups, wait_all_groups=()):
        ops = self.ops
        per = {e: [] for e in ['pe', 'act', 'dve', 'pool', 'sp']}
        for op in ops:
            per[op['eng']].append(op)
        gtotal = self.gtotal

        def body(eng, h):
            waited = {}
            for op in per[eng]:
                need = {}
                for d in op['deps']:
                    dop = ops[d]
                    if dop['dma'] is None:
                        if dop['eng'] == 'pe' and eng == 'pe' and op['dma'] is None:
                            continue
                    key, val = dop['sig']
                    if key[0] == 'g' and key[1] in wait_all_groups:
                        val = gtotal[key[1]]
                    if need.get(key, 0) < val:
                        need[key] = val
                for key, val in need.items():
                    if waited.get(key, 0) >= val:
                        continue
                    h.wait_ge(sems[key], val)
                    waited[key] = val
                ins = op['fn'](h)
                if op['dma'] is not None:
                    ins.then_inc(sems[op['sig'][0]], 16)
                elif op['signal']:
                    ins.then_inc(sems[op['sig'][0]], 1)
            if eng == 'sp':
                for g in final_groups:
                    if g in gtotal:
                        h.wait_ge(sems[('g', g)], gtotal[g])
        return body


class Buf:
    def __init__(self, nc, name, fshape, dtype, region, roff, boff):
        self.t = nc.alloc_sbuf_tensor_at(name, [128] + list(fshape), dtype, offset=roff + boff)
        self.es = 4 if dtype == F32 else 2
        self.fshape = list(fshape)
        self.region = region
        self.rb = boff
        self.C = fshape[-1]
        self.nbytes = self.es * int(np.prod(fshape))

    def iv(self, planes, c0, c1):
        lin = 0
        for dsz, i in zip(self.fshape[:-1], planes):
            lin = lin * dsz + i
        b = self.rb + lin * self.C * self.es
        return (self.region, b + c0 * self.es, b + c1 * self.es)

    def all(self):
        return (self.region, self.rb, self.rb + self.nbytes)


def t5_bucket_np(dist):
    n = np.maximum(dist, 0)
    max_exact = 16
    nf = np.maximum(n, 1).astype(np.float32)
    large = max_exact + (np.log(nf / np.float32(max_exact)) / np.float32(math.log(128 / max_exact))
                         * np.float32(32 - max_exact)).astype(np.int32)
    large = np.minimum(large, 31)
    return np.where(n < max_exact, n, large)


def build_program(max_ops=None, marks=None):
    nc = bass.Bass("TRN2", target_bir_lowering=False, dynamic_dma_scratch_size=8192)
    S = Sched()
    MK = {}

    def dram_in(name, shape):
        return nc.dram_tensor(name, list(shape), F32, kind="ExternalInput").ap()

    def dram_out(name, shape):
        return nc.dram_tensor(name, list(shape), F32, kind="ExternalOutput").ap()

    xT = dram_in("xT", [D, NT])
    scT = dram_in("scT", [D, 16, 2])
    ck = dram_in("ck", [16, 128, 128])
    cv = dram_in("cv", [16, 128, 128])
    gall = dram_in("gall", [128, 40])
    cwd = dram_in("cwd", [128, 24])
    sinks = dram_in("sinks", [1, 16])
    rel = dram_in("rel", [128, 16])
    onehot = dram_in("onehot", [128, 384])
    ident_d = dram_in("ident", [128, 128])
    cmask_d = dram_in("cmask", [128, 1])
    w_conv_in = dram_in("w_conv_in", [D, 3 * D])
    w_conv_out = dram_in("w_conv_out", [D, D])
    w_q = dram_in("w_q", [D, D])
    wkpad = dram_in("wkpad", [4, D, 128])
    w_k = dram_in("w_k", [D, 128])
    w_v = dram_in("w_v", [D, 128])
    w_o = dram_in("w_o", [D, D])
    w_gate = dram_in("w_gate", [2, D, DFF])
    w_up = dram_in("w_up", [2, D, DFF])
    w_down = dram_in("w_down", [2, DFF, D])

    yT = dram_out("yT", [D, NQ])
    uT = dram_out("uT", [D, 66])
    kvp = dram_out("kvp", [2, 128, 128])
    cks = dram_out("cks", [16, 128, 128])
    cvs = dram_out("cvs", [16, 128, 128])
    Bscr_t = nc.dram_tensor("Bscr", [16, 128, 384], F32, kind="Internal")
    Bscr = Bscr_t.ap()

    base = 8320
    regions = {}
    cur = [base]

    def region(name, size):
        size = (size + 31) // 32 * 32
        regions[name] = (cur[0], size)
        cur[0] += size
        return cur[0] - size

    rX = region('X', NCH * NT * 4)
    rH = region('H', NCH * NT * 2)
    rS = region('S', NCH * NT * 2)
    rW = region('W', NW * 2048)
    rT = region('T', 18976)
    rN = region('N', 6336)
    rV = region('V', 18 * 320 * 2 + 64)
    rB = region('B', 3 * 4096)
    rC = region('C', 10496)
    assert cur[0] <= 229344, cur[0]

    def mk(name, fshape, dtype, rname, boff):
        roff = regions[rname][0]
        b = Buf(nc, name, fshape, dtype, rname, roff, boff)
        assert boff + b.nbytes <= regions[rname][1], (name, boff, b.nbytes, regions[rname])
        return b

    X = mk("X", [NCH, NT], F32, 'X', 0)
    H = mk("H", [NCH, NT], BF16, 'H', 0)
    SB = mk("SB", [NCH, NT], BF16, 'S', 0)
    WR = mk("WR", [NW, 8, 128], BF16, 'W', 0)
    UB = mk("UB", [2, 2180], BF16, 'T', 0)
    US = mk("US", [2, 16, 6], BF16, 'T', 8736)
    CSB = mk("CSB", [2, 450], F32, 'T', 9120)
    BSB = mk("BSB", [3, 450], F32, 'T', 12736)
    KP = mk("KP", [4, NT], BF16, 'T', 0)
    SQ = mk("SQ", [5, 450], BF16, 'N', 0)
    TMP = mk("TMP", [1, 450], F32, 'N', 4512)
    VA = mk("VA", [18, 320], BF16, 'V', 0)
    ACC = mk("ACC", [3, 450], F32, 'V', 0)
    SG = mk("SG", [3, 450], F32, 'V', 6144)
    TB = mk("TB", [3, 16, 128], BF16, 'B', 0)
    off = [0]

    def cmk(name, fshape, dtype):
        b = mk(name, fshape, dtype, 'C', off[0])
        off[0] += (b.nbytes + 31) // 32 * 32
        return b

    GA = cmk("GA", [40], F32)
    CW = cmk("CW", [24], F32)
    ONES = cmk("ONES", [128], BF16)
    IDB = cmk("IDB", [128], BF16)
    ESEL = cmk("ESEL", [2, 128], BF16)
    ESR = cmk("ESR", [16, 128], BF16)
    CM = cmk("CM", [1], F32)
    U32 = cmk("U32", [NCH, 66], F32)
    SCB = cmk("SCB", [NCH, 16, 2], BF16)
    RELT = cmk("RELT", [16], F32)
    EF = cmk("EF", [16], F32)
    EHI = cmk("EHI", [16], BF16)
    ELO = cmk("ELO", [16], BF16)
    SNK = cmk("SNK", [16], F32)
    ESN = cmk("ESN", [16], F32)
    KST = cmk("KST", [2, 128], F32)
    VST = cmk("VST", [2, 128], F32)
    EBC = mk("EBC", [2, 16, 128], BF16, 'S', 13472)
    REP = mk("REP", [8, 384], F32, 'S', 21664)
    OHB = mk("OHB", [384], BF16, 'S', 33952)
    PB = mk("PB", [4, 512], BF16, 'H', 0)
    REC = mk("REC", [2, 512], F32, 'H', 6144)
    KC2 = mk("KC2", [16, 2, 128], BF16, 'H', 10240)
    VAS = mk("VAS", [16, 320], BF16, 'H', 18432)
    KCT = mk("KCT", [4, 4, 128], BF16, 'N', 2048)
    OWN = mk("OWN", [4, 16, 4, 4], F32, 'H', 30720)
    PP0 = mk("PP0", [640], BF16, 'H', 4096)
    PP1 = mk("PP1", [640], BF16, 'H', 28672)
    PO = mk("PO", [512], BF16, 'H', 34816)
    RCS = mk("RCS", [1, 256], F32, 'H', 30720)
    YST = mk("YST", [2, NCH, 424], F32, 'H', 0)
    OWNB = mk("OWNB", [1024], BF16, 'N', 0)

    ps_t = nc.alloc_psum_tensor("ps", [128, 8, 512], F32)

    def psiv(bank, c0, c1):
        return ('ps', bank * 2048, (bank + 1) * 2048)

    bank_ctr = [0]

    def nb():
        b = bank_ctr[0] % 8
        bank_ctr[0] += 1
        return b

    ps = ps_t

    pieces = []

    def wview(ap2d, col0):
        return ap2d.rearrange("(k p) n -> p k n", p=128)[:, :, col0:col0 + 128]

    wstate = dict(loaded=0)

    def add_piece(src, npl=8):
        pieces.append((src, npl))
        return len(pieces) - 1

    def ensure_loaded(upto):
        upto = min(upto, len(pieces) - 1)
        while wstate['loaded'] <= upto:
            n = wstate['loaded']
            src, npl = pieces[n]
            slot = n % NW
            dst = WR.t[:, slot, 0:npl, :]
            S.emit('pool', (lambda d, s: (lambda e: e.dma_start(out=d, in_=s)))(dst, src),
                   writes=[WR.iv((slot, 0), 0, npl * 128)], dma='w%d' % slot)
            wstate['loaded'] += 1

    piece_ctr = [0]

    def use_piece():
        n = piece_ctr[0]
        piece_ctr[0] += 1
        assert n < wstate['loaded'], (n, wstate['loaded'])
        return n % NW

    def release_upto(n):
        ensure_loaded(n + NW)

    def release_all():
        ensure_loaded(piece_ctr[0] - 1 + NW)

    for m in range(NCH):
        add_piece(wview(w_conv_in, D + m * 128))
        add_piece(wview(w_conv_in, 2 * D + m * 128))
        add_piece(wview(w_conv_in, m * 128))
    for m in range(NCH):
        add_piece(wview(w_conv_out, m * 128))

    def ffn_pieces(l):
        for grp in FF_GROUPS:
            for j in grp:
                add_piece(wview(w_gate[l], j * 128))
                add_piece(wview(w_up[l], j * 128))
            for m in range(NCH):
                src = w_down[l][grp[0] * 128:(grp[-1] + 1) * 128, m * 128:(m + 1) * 128].rearrange(
                    "(jj p) c -> p jj c", p=128)
                add_piece(src, len(grp))
    ffn_pieces(0)
    for v in range(4):
        add_piece(wview(wkpad[v], 0))
    add_piece(wview(w_v, 0))
    add_piece(wview(w_k, 0))
    for m in range(NCH):
        add_piece(wview(w_q, m * 128))
    for m in range(NCH):
        add_piece(wview(w_o, m * 128))
    ffn_pieces(1)

    class _Rec:
        def __getattr__(self, name):
            def f(*a, **k):
                self.call = (name, a, k)
                return self
            return f

    def E(eng, fn, reads=(), writes=(), dma=None):
        r = _Rec()
        fn(r)
        name, a, k = r.call
        return S.emit(eng, lambda h: getattr(h, name)(*a, **k), reads, writes, dma)

    def mm(out, lhsT, rhs, start, stop, reads, writes):
        E('pe', lambda e: e.matmul(out, lhsT=lhsT, rhs=rhs, start=start, stop=stop), reads, writes)

    def cload(dst_buf, dst_ap, src_ap, eng='sp'):
        E(eng, lambda e: e.dma_start(out=dst_ap, in_=src_ap), writes=[dst_buf.all()],
          dma='const' if eng == 'sp' else 'constp')

    xv = xT.rearrange("(k p) c -> p k c", p=128)
    T_in = [(0, 450), (450, 898), (898, 1346), (1346, 1794), (1794, 2242)]
    def xload(si):
        c0, c1 = T_in[si]
        E('sp', lambda e: e.dma_start(out=X.t[:, :, c0:c1], in_=xv[:, :, c0:c1]),
          reads=([('XORD', si - 2, si - 1)] if si >= 2 else []),
          writes=[X.iv((k,), c0, c1) for k in range(NCH)] + [('XORD', si, si + 1)], dma='x%d' % si)

    xload(0)
    xload(1)
    cload(GA, GA.t[:, :], gall[:, :])
    cload(CW, CW.t[:, :], cwd[:, :])
    cload(CM, CM.t[:, :], cmask_d[:, :])
    cload(RELT, RELT.t[:, :], rel[:, :])
    cload(SNK, SNK.t[0:1, :], sinks[:, :])
    cload(IDB, IDB.t[:, :], ident_d[:, :], 'pool')
    cload(OHB, OHB.t[:, :], onehot[:, :], 'pool')
    cload(SCB, SCB.t[:, :, :, :], scT.rearrange("(k p) b t -> p k b t", p=128), 'pool')
    for si_ in range(2, len(T_in)):
        xload(si_)
    ensure_loaded(NW - 1)

    E('dve', lambda e: e.memset(ONES.t[:, :], 1.0), writes=[ONES.all()])
    E('act', lambda e: e.activation(out=SQ.t[:, 4, 0:2], in_=ONES.t[:, 0:2], func=AF.Square),
      reads=[ONES.all()], writes=[SQ.iv((4,), 0, 2)])

    def build_consts():
        E('dve', lambda e: e.memset(ESEL.t[:, :, :], 0.0), writes=[ESEL.all()])
        E('dve', lambda e: e.memset(ESEL.t[0:1, 0, 64:128], 1.0), writes=[ESEL.all()])
        E('dve', lambda e: e.memset(ESEL.t[0:1, 1, 0:64], 1.0), writes=[ESEL.all()])
        E('dve', lambda e: e.memset(ESR.t[:, :, :], 0.0), writes=[ESR.all()])
        E('act', lambda e: e.activation(out=ESN.t[0:1, :], in_=SNK.t[0:1, :], func=AF.Exp),
          reads=[SNK.all()], writes=[ESN.all()])
        for v in range(4):
            g, p = v // 2, v % 2
            src = ESN.t[0:1, :].rearrange("o (g hq q) -> o g hq q", g=2, hq=4, q=2)[:, g, :, p]
            E('dve', (lambda vv, s: (lambda e: e.tensor_copy(
                out=ESR.t[0:1, vv * 4:(vv + 1) * 4, :], in_=s.unsqueeze(2).to_broadcast([1, 4, 128]))))(v, src),
              reads=[ESN.all()], writes=[ESR.all()])
        MK['startup_done'] = len(S.ops)

    def bias_prep_dve():
        E('act', lambda e: e.activation(out=EF.t[:, :], in_=RELT.t[:, :], func=AF.Copy, scale=8.0),
          reads=[RELT.all()], writes=[EF.all()])
        E('dve', lambda e: e.tensor_copy(out=EHI.t[:, :], in_=EF.t[:, :]), reads=[EF.all()], writes=[EHI.all()])
        E('dve', lambda e: e.tensor_tensor(out=ELO.t[:, :], in0=EF.t[:, :], in1=EHI.t[:, :], op=ALU.subtract),
          reads=[EF.all(), EHI.all()], writes=[ELO.all()])
        for xi_, EB in enumerate((EHI, ELO)):
            for v in range(4):
                g, p = v // 2, v % 2
                src = EB.t[:, :].rearrange("o (g hq q) -> o g hq q", g=2, hq=4, q=2)[:, g, :, p]
                E('dve', lambda e: e.tensor_copy(
                    out=EBC.t[:, xi_, v * 4:(v + 1) * 4, :], in_=src.unsqueeze(2).to_broadcast([128, 4, 128])),
                  reads=[EB.all()], writes=[EBC.iv((xi_, v * 4), 0, 512)])

    def bias_items():
        for hq in range(16):
            bk = nb()
            mm(ps[:, bk, 0:384], EBC.t[:, 0, hq, :], OHB.t[:, :], True, False,
               [EBC.iv((0, hq), 0, 128), OHB.all()], [psiv(bk, 0, 384)])
            mm(ps[:, bk, 0:384], EBC.t[:, 1, hq, :], OHB.t[:, :], False, True,
               [EBC.iv((1, hq), 0, 128), OHB.all()], [psiv(bk, 0, 384)])
            E('act', lambda e: e.activation(out=REP.t[:, hq % 8, :], in_=ps[:, bk, 0:384], func=AF.Copy),
              reads=[psiv(bk, 0, 384)], writes=[REP.iv((hq % 8,), 0, 384)])
            if hq % 8 == 7:
                hb = hq // 8
                E('sp', lambda e: e.dma_start(out=Bscr.rearrange("h p c -> p h c")[:, hb * 8:(hb + 1) * 8, :],
                                              in_=REP.t[:, :, :]),
                  reads=[REP.all()], writes=[('Bscr', hb, hb + 1)], dma='bscr')
            yield
        MK['bias_done'] = len(S.ops)

    def load_bias_tiles():
        for ti, offv in ((0, 127), (1, 255)):
            src = bass.AP(Bscr_t, offv, [[383, 128], [128 * 384, 16], [1, 128]])
            E('pool', (lambda t_, s: (lambda e: e.dma_start(out=TB.t[:, t_, :, :], in_=s)))(ti, src),
              reads=[('Bscr', 0, 2)], writes=[TB.iv((ti, 0), 0, 2048)], dma='tb%d' % ti)

    def make_tfirst():
        E('dve', lambda e: e.tensor_scalar(out=TB.t[:, 2, :, :], in0=TB.t[:, 1, :, :], scalar1=CM.t[:, 0:1],
                                            scalar2=None, op0=ALU.add),
          reads=[TB.iv((1, 0), 0, 2048), CM.all()], writes=[TB.iv((2, 0), 0, 2048)])


    sq_ctr = [0]
    tmp_ctr = [0]

    def emit_norm(ni, tiles, final=False):
        for ti, (c0, c1) in enumerate(tiles):
            norm_tile(ni, ti, c0, c1, final)

    def norm_tile(ni, ti, c0, c1, final=False):
        if True:
            n = c1 - c0
            bk = nb()
            for k in range(NCH):
                sl = sq_ctr[0] % 5
                sq_ctr[0] += 1
                E('act', (lambda k_, s_: (lambda e: e.activation(out=SQ.t[:, s_, 0:n], in_=X.t[:, k_, c0:c1],
                                                                   func=AF.Square)))(k, sl),
                  reads=[X.iv((k,), c0, c1)], writes=[SQ.iv((sl,), 0, n)])
                mm(ps[:, bk, 0:n], ONES.t[:, :], SQ.t[:, sl, 0:n], k == 0, k == NCH - 1,
                   [ONES.all(), SQ.iv((sl,), 0, n)], [psiv(bk, 0, n)])
            ts = 0
            tmp_ctr[0] += 1
            E('act', lambda e: e.activation(out=TMP.t[:, ts, 0:n], in_=ps[:, bk, 0:n], func=AF.Ln,
                                            scale=1.0 / D, bias=EPS),
              reads=[psiv(bk, 0, n)], writes=[TMP.iv((ts,), 0, n)])
            E('act', lambda e: e.activation(out=ps[:, bk, 0:n], in_=TMP.t[:, ts, 0:n], func=AF.Exp, scale=-0.5),
              reads=[TMP.iv((ts,), 0, n)], writes=[psiv(bk, 0, n)])
            if not final:
                for k in range(NCH):
                    E('dve', (lambda k_: (lambda e: e.scalar_tensor_tensor(
                        out=H.t[:, k_, c0:c1], in0=X.t[:, k_, c0:c1], scalar=GA.t[:, ni * 8 + k_:ni * 8 + k_ + 1],
                        in1=ps[:, bk, 0:n], op0=ALU.mult, op1=ALU.mult)))(k),
                      reads=[X.iv((k,), c0, c1), GA.all(), psiv(bk, 0, n)], writes=[H.iv((k,), c0, c1)])
            else:
                ys = ti % 2
                yv = yT.rearrange("(k p) c -> p k c", p=128)
                for k in range(NCH):
                    E('dve', lambda e: e.scalar_tensor_tensor(
                        out=YST.t[:, ys, k, 0:n], in0=X.t[:, k, c0:c1], scalar=GA.t[:, ni * 8 + k:ni * 8 + k + 1],
                        in1=ps[:, bk, 0:n], op0=ALU.mult, op1=ALU.mult),
                      reads=[X.iv((k,), c0, c1), GA.all(), psiv(bk, 0, n)], writes=[YST.iv((ys, k), 0, n)])
                    E('sp', lambda e: e.dma_start(out=yv[:, k, c0 - C_P0:c1 - C_P0], in_=YST.t[:, ys, k, 0:n]),
                      reads=[YST.iv((ys, k), 0, n)], dma='y%d_%d' % (ys, k))

    norm_tile(0, 0, T_in[0][0], T_in[0][1])
    build_consts()
    bias_prep_dve()
    bgen = bias_items()
    norm_tile(0, 1, T_in[1][0], T_in[1][1])

    MK['norm0_done'] = len(S.ops)
    csb_ctr = [0]
    bsb_ctr = [0]
    acc_ctr = [0]
    pending = [None]

    def conv_step(m, c0, c1, bs):
        z0 = max(c0, 2)
        z1 = c1
        zt1 = min(z1, C_S0)
        nt = zt1 - z0
        ub = m % 2
        bsl = acc_ctr[0] % 3
        acc_ctr[0] += 1
        nz = z1 - z0
        for j in range(3):
            wj = CW.t[:, j * 8 + m:j * 8 + m + 1]
            src = UB.t[:, ub, z0 - 2 + j:zt1 - 2 + j]
            if j == 0:
                E('dve', lambda e: e.tensor_scalar(out=ACC.t[:, bsl, 0:nt], in0=src, scalar1=wj, scalar2=None,
                                                   op0=ALU.mult),
                  reads=[UB.iv((ub,), z0 - 2 + j, zt1 - 2 + j), CW.all()], writes=[ACC.iv((bsl,), 0, nt)])
            else:
                E('dve', lambda e: e.scalar_tensor_tensor(out=ACC.t[:, bsl, 0:nt], in0=src, scalar=wj,
                                                          in1=ACC.t[:, bsl, 0:nt], op0=ALU.mult, op1=ALU.add),
                  reads=[UB.iv((ub,), z0 - 2 + j, zt1 - 2 + j), CW.all(), ACC.iv((bsl,), 0, nt)],
                  writes=[ACC.iv((bsl,), 0, nt)])
        if z1 > C_S0:
            acc = ACC.t[:, bsl, nt:nt + 64].rearrange("p (b t) -> p b t", t=4)
            for j in range(3):
                wj = CW.t[:, j * 8 + m:j * 8 + m + 1]
                src = US.t[:, ub, :, j:j + 4]
                if j == 0:
                    E('dve', lambda e: e.tensor_scalar(out=acc, in0=src, scalar1=wj, scalar2=None, op0=ALU.mult),
                      reads=[US.iv((ub, 0), 0, 96), CW.all()], writes=[ACC.iv((bsl,), nt, nt + 64)])
                else:
                    E('dve', lambda e: e.scalar_tensor_tensor(out=acc, in0=src, scalar=wj, in1=acc,
                                                              op0=ALU.mult, op1=ALU.add),
                      reads=[US.iv((ub, 0), 0, 96), CW.all(), ACC.iv((bsl,), nt, nt + 64)],
                      writes=[ACC.iv((bsl,), nt, nt + 64)])
        E('dve', lambda e: e.tensor_tensor(out=SB.t[:, m, z0:z1], in0=ACC.t[:, bsl, 0:nz],
                                           in1=BSB.t[:, bs, z0 - c0:z1 - c0], op=ALU.mult),
          reads=[ACC.iv((bsl,), 0, nz), BSB.iv((bs,), z0 - c0, z1 - c0)], writes=[SB.iv((m,), z0, z1)])

    for mgrp in ((0, 1), (2,), (3,), (4,), (5,), (6,), (7,)):
        sls = {}
        for m in mgrp:
            sls[m] = (use_piece(), use_piece(), use_piece())
            E('dve', lambda e: e.tensor_copy(out=US.t[:, m % 2, :, 0:2], in_=SCB.t[:, m, :, :]),
              reads=[SCB.all()], writes=[US.iv((m % 2, 0), 0, 96)])
        for si, (c0, c1) in enumerate(T_in):
            if mgrp == (2,):
                for _ in range(3):
                    next(bgen, None)
            if mgrp == (3,) and si == 0:
                next(bgen, None)
            if mgrp == (0, 1) and si + 2 < len(T_in):
                norm_tile(0, si + 2, T_in[si + 2][0], T_in[si + 2][1])
            for m in mgrp:
                ub = m % 2
                n = c1 - c0
                bc, bx, bb = nb(), nb(), nb()
                for (bk, sl) in ((bc, sls[m][0]), (bx, sls[m][1]), (bb, sls[m][2])):
                    for k in range(NCH):
                        mm(ps[:, bk, 0:n], WR.t[:, sl, k, :], H.t[:, k, c0:c1], k == 0, k == NCH - 1,
                           [WR.iv((sl, k), 0, 128), H.iv((k,), c0, c1)], [psiv(bk, 0, n)])
                cs = csb_ctr[0] % 2
                csb_ctr[0] += 1
                E('act', lambda e: e.activation(out=CSB.t[:, cs, 0:n], in_=ps[:, bc, 0:n], func=AF.Copy),
                  reads=[psiv(bc, 0, n)], writes=[CSB.iv((cs,), 0, n)])
                t1 = min(c1, C_S0)
                E('dve', lambda e: e.tensor_tensor(out=UB.t[:, ub, c0:t1], in0=CSB.t[:, cs, 0:t1 - c0],
                                                   in1=ps[:, bx, 0:t1 - c0], op=ALU.mult),
                  reads=[CSB.iv((cs,), 0, t1 - c0), psiv(bx, 0, t1 - c0)], writes=[UB.iv((ub,), c0, t1)])
                if c1 > C_S0:
                    o0 = C_S0 - c0
                    E('dve', lambda e: e.tensor_tensor(
                        out=US.t[:, ub, :, 2:6], in0=CSB.t[:, cs, o0:o0 + 64].rearrange("p (b t) -> p b t", t=4),
                        in1=ps[:, bx, o0:o0 + 64].rearrange("p (b t) -> p b t", t=4), op=ALU.mult),
                      reads=[CSB.iv((cs,), o0, o0 + 64), psiv(bx, o0, o0 + 64)], writes=[US.iv((ub, 0), 0, 96)])
                    o1 = C_S0 - 2 - c0
                    E('dve', lambda e: e.tensor_tensor(out=U32.t[:, m, :], in0=CSB.t[:, cs, o1:o1 + 66],
                                                       in1=ps[:, bx, o1:o1 + 66], op=ALU.mult),
                      reads=[CSB.iv((cs,), o1, o1 + 66), psiv(bx, o1, o1 + 66)], writes=[U32.iv((m,), 0, 66)])
                bs = bsb_ctr[0] % 3
                bsb_ctr[0] += 1
                E('act', lambda e: e.activation(out=BSB.t[:, bs, 0:n], in_=ps[:, bb, 0:n], func=AF.Copy),
                  reads=[psiv(bb, 0, n)], writes=[BSB.iv((bs,), 0, n)])
                if pending[0] is not None:
                    conv_step(*pending[0])
                pending[0] = (m, c0, c1, bs)
        release_all()
    conv_step(*pending[0])
    pending[0] = None
    E('sp', lambda e: e.dma_start(out=cks[:, 0:124, :], in_=ck[:, 4:128, :]), dma='outc')
    E('sp', lambda e: e.dma_start(out=cvs[:, 0:124, :], in_=cv[:, 4:128, :]), dma='outc')
    for _ in bgen:
        pass
    load_bias_tiles()
    uv = uT.rearrange("(k p) c -> p k c", p=128)
    E('sp', lambda e: e.dma_start(out=uv[:, :, :], in_=U32.t[:, :, :]), reads=[U32.all()], dma='outc')

    MK['convin_done'] = len(S.ops)
    A_x = split(C_H0, NT)
    A_q = split(C_P0, NT)

    deferred = []

    def run_deferred():
        while deferred:
            deferred.pop(0)()

    def proj_add(src_buf, tiles, next_norm):
        n0 = piece_ctr[0]
        slots = [use_piece() for _ in range(NCH)]
        ni, fin = next_norm
        for si, (c0, c1) in enumerate(tiles):
            n = c1 - c0
            for m in range(NCH):
                sl = slots[m]
                bk = nb()
                for k in range(NCH):
                    mm(ps[:, bk, 0:n], WR.t[:, sl, k, :], src_buf.t[:, k, c0:c1], k == 0, k == NCH - 1,
                       [WR.iv((sl, k), 0, 128), src_buf.iv((k,), c0, c1)], [psiv(bk, 0, n)])
                E('dve', lambda e: e.tensor_tensor(out=X.t[:, m, c0:c1], in0=X.t[:, m, c0:c1],
                                                   in1=ps[:, bk, 0:n], op=ALU.add),
                  reads=[X.iv((m,), c0, c1), psiv(bk, 0, n)], writes=[X.iv((m,), c0, c1)])
                if si == len(tiles) - 1:
                    release_upto(n0 + m)
            if si >= 1:
                norm_tile(ni, si - 1, tiles[si - 1][0], tiles[si - 1][1], fin)
        if fin:
            norm_tile(ni, len(tiles) - 1, tiles[-1][0], tiles[-1][1], fin)
        else:
            deferred.append(lambda: norm_tile(ni, len(tiles) - 1, tiles[-1][0], tiles[-1][1], fin))

    proj_add(SB, A_x, (1, False))

    MK['convout_done'] = len(S.ops)
    sg_ctr = [0]

    def emit_ffn(tiles, next_norm):
        def gu_item(jj, slg, slu, c0, c1):
            n = c1 - c0
            bg, bu = nb(), nb()
            for (bk, sl) in ((bg, slg), (bu, slu)):
                for k in range(NCH):
                    mm(ps[:, bk, 0:n], WR.t[:, sl, k, :], H.t[:, k, c0:c1], k == 0, k == NCH - 1,
                       [WR.iv((sl, k), 0, 128), H.iv((k,), c0, c1)], [psiv(bk, 0, n)])
            ss = sg_ctr[0] % 3
            sg_ctr[0] += 1
            E('act', lambda e: e.activation(out=SG.t[:, ss, 0:n], in_=ps[:, bg, 0:n], func=AF.Silu),
              reads=[psiv(bg, 0, n)], writes=[SG.iv((ss,), 0, n)])
            E('dve', lambda e: e.tensor_tensor(out=SB.t[:, jj, c0:c1], in0=SG.t[:, ss, 0:n],
                                               in1=ps[:, bu, 0:n], op=ALU.mult),
              reads=[SG.iv((ss,), 0, n), psiv(bu, 0, n)], writes=[SB.iv((jj,), c0, c1)])

        for grp in FF_GROUPS:
            jj0 = 0
            if grp is FF_GROUPS[0] and deferred:
                sl = [(use_piece(), use_piece()) for _ in range(2)]
                T = tiles
                gu_item(0, sl[0][0], sl[0][1], *T[0])
                gu_item(0, sl[0][0], sl[0][1], *T[1])
                run_deferred()
                for t in T[2:-1]:
                    gu_item(0, sl[0][0], sl[0][1], *t)
                for t in T[:-1]:
                    gu_item(1, sl[1][0], sl[1][1], *t)
                gu_item(0, sl[0][0], sl[0][1], *T[-1])
                gu_item(1, sl[1][0], sl[1][1], *T[-1])
                release_all()
                jj0 = 2
            for jj in range(jj0, len(grp)):
                sl_g = use_piece()
                sl_u = use_piece()
                for (c0, c1) in tiles:
                    gu_item(jj, sl_g, sl_u, c0, c1)
                release_all()
            ng = len(grp)
            last = grp is FF_GROUPS[-1]
            if not last:
                for m in range(NCH):
                    sl = use_piece()
                    for (c0, c1) in tiles:
                        n = c1 - c0
                        bk = nb()
                        for jj in range(ng):
                            mm(ps[:, bk, 0:n], WR.t[:, sl, jj, :], SB.t[:, jj, c0:c1], jj == 0, jj == ng - 1,
                               [WR.iv((sl, jj), 0, 128), SB.iv((jj,), c0, c1)], [psiv(bk, 0, n)])
                        E('dve', lambda e: e.tensor_tensor(out=X.t[:, m, c0:c1], in0=X.t[:, m, c0:c1],
                                                           in1=ps[:, bk, 0:n], op=ALU.add),
                          reads=[X.iv((m,), c0, c1), psiv(bk, 0, n)], writes=[X.iv((m,), c0, c1)])
                    release_all()
            else:
                n0 = piece_ctr[0]
                slots = [use_piece() for _ in range(NCH)]
                ni, fin = next_norm
                if fin:
                    lc0, lc1 = tiles[-1]
                    tiles = tiles[:-1] + [(lc0, lc1 - 120), (lc1 - 120, lc1)]
                for si, (c0, c1) in enumerate(tiles):
                    n = c1 - c0
                    for m in range(NCH):
                        sl = slots[m]
                        bk = nb()
                        for jj in range(ng):
                            mm(ps[:, bk, 0:n], WR.t[:, sl, jj, :], SB.t[:, jj, c0:c1], jj == 0, jj == ng - 1,
                               [WR.iv((sl, jj), 0, 128), SB.iv((jj,), c0, c1)], [psiv(bk, 0, n)])
                        E('dve', lambda e: e.tensor_tensor(out=X.t[:, m, c0:c1], in0=X.t[:, m, c0:c1],
                                                           in1=ps[:, bk, 0:n], op=ALU.add),
                          reads=[X.iv((m,), c0, c1), psiv(bk, 0, n)], writes=[X.iv((m,), c0, c1)])
                        if si == len(tiles) - 1:
                            release_upto(n0 + m)
                    if si >= 1:
                        norm_tile(ni, si - 1, tiles[si - 1][0], tiles[si - 1][1], fin)
                if fin:
                    norm_tile(ni, len(tiles) - 1, tiles[-1][0], tiles[-1][1], fin)
                else:
                    deferred.append(lambda: norm_tile(ni, len(tiles) - 1, tiles[-1][0], tiles[-1][1], fin))

    emit_ffn(A_x, (2, False))

    MK['ffn0_done'] = len(S.ops)
    def kp_item(v, sl, c0, c1):
        n = c1 - c0
        bk = nb()
        for k in range(NCH):
            mm(ps[:, bk, 0:n], WR.t[:, sl, k, :], H.t[:, k, c0:c1], k == 0, k == NCH - 1,
               [WR.iv((sl, k), 0, 128), H.iv((k,), c0, c1)], [psiv(bk, 0, n)])
        E('act', lambda e: e.activation(out=KP.t[:, v, c0:c1], in_=ps[:, bk, 0:n], func=AF.Copy),
          reads=[psiv(bk, 0, n)], writes=[KP.iv((v,), c0, c1)])

    slk = [use_piece() for _ in range(4)]
    kp_item(0, slk[0], *A_x[0])
    kp_item(0, slk[0], *A_x[1])
    kp_item(1, slk[1], *A_x[0])
    kp_item(1, slk[1], *A_x[1])
    run_deferred()
    for v in range(4):
        for t in (A_x[2:-1] if v < 2 else A_x[:-1]):
            kp_item(v, slk[v], *t)
    for v in range(4):
        kp_item(v, slk[v], *A_x[-1])
    release_all()
    MK['kpad_done'] = len(S.ops)
    E('dve', lambda e: e.memset(VA.t[:, :, :], 1.0), writes=[VA.all()])
    blocks = [(C_H0 + 128 * i, C_H0 + 128 * (i + 1)) for i in range(17)] + [(NT - 128, NT)]
    sl_v = use_piece()
    for bi, (b0, b1) in enumerate(blocks):
        bk = nb()
        for k in range(NCH):
            mm(ps[:, bk, 0:128], H.t[:, k, b0:b1], WR.t[:, sl_v, k, :], k == 0, k == NCH - 1,
               [WR.iv((sl_v, k), 0, 128), H.iv((k,), b0, b1)], [psiv(bk, 0, 128)])
        E('dve', (lambda bi_: (lambda e: e.tensor_copy(
            out=VA.t[:, bi_, 64:320].rearrange("p (a c) -> p a c", c=128)[:, :, 0:64],
            in_=ps[:, bk, 0:128].rearrange("p (a c) -> p a c", c=64))))(bi),
          reads=[psiv(bk, 0, 128)], writes=[VA.iv((bi,), 0, 320)])
        if bi >= 16:
            vs = bi - 16
            E('dve', lambda e: e.tensor_copy(out=VST.t[:, vs, :], in_=ps[:, bk, 0:128]),
              reads=[psiv(bk, 0, 128)], writes=[VST.iv((vs,), 0, 128)])
            if bi == 16:
                E('sp', lambda e: e.dma_start(out=kvp[1, :, :], in_=VST.t[:, 0, :]),
                  reads=[VST.iv((0,), 0, 128)], dma='outc')
            else:
                for b in range(16):
                    E('sp', (lambda b_: (lambda e: e.dma_start(out=cvs[b_, 124:128, :],
                                                                in_=VST.t[64 + 4 * b_:68 + 4 * b_, 1, :])))(b),
                      reads=[VST.iv((1,), 0, 128)], dma='outc')
    MK['vtok_done'] = len(S.ops)
    release_all()
    sl_k = use_piece()
    for bi in (16, 17):
        b0, b1 = blocks[bi]
        bk = nb()
        for k in range(NCH):
            mm(ps[:, bk, 0:128], H.t[:, k, b0:b1], WR.t[:, sl_k, k, :], k == 0, k == NCH - 1,
               [WR.iv((sl_k, k), 0, 128), H.iv((k,), b0, b1)], [psiv(bk, 0, 128)])
        ks = bi - 16
        E('act', lambda e: e.activation(out=KST.t[:, ks, :], in_=ps[:, bk, 0:128], func=AF.Copy),
          reads=[psiv(bk, 0, 128)], writes=[KST.iv((ks,), 0, 128)])
        if bi == 16:
            E('sp', lambda e: e.dma_start(out=kvp[0, :, :], in_=KST.t[:, 0, :]),
              reads=[KST.iv((0,), 0, 128)], dma='outc')
        else:
            for b in range(16):
                E('sp', (lambda b_: (lambda e: e.dma_start(out=cks[b_, 124:128, :],
                                                            in_=KST.t[64 + 4 * b_:68 + 4 * b_, 1, :])))(b),
                  reads=[KST.iv((1,), 0, 128)], dma='outc')
    MK['ktok_done'] = len(S.ops)
    release_all()
    for m in range(NCH):
        sl = use_piece()
        for (c0, c1) in A_q:
            n = c1 - c0
            bk = nb()
            for k in range(NCH):
                mm(ps[:, bk, 0:n], WR.t[:, sl, k, :], H.t[:, k, c0:c1], k == 0, k == NCH - 1,
                   [WR.iv((sl, k), 0, 128), H.iv((k,), c0, c1)], [psiv(bk, 0, n)])
            E('act', (lambda m_: (lambda e: e.activation(out=SB.t[:, m_, c0:c1], in_=ps[:, bk, 0:n], func=AF.Copy)))(m),
              reads=[psiv(bk, 0, n)], writes=[SB.iv((m,), c0, c1)])
        release_all()

    MK['qkv_done'] = len(S.ops)
    VSL = {0: (64, 192), 1: (0, 128), 2: (192, 320), 3: (128, 256)}
    def sample_loads():
        ckv = ck.rearrange("b k c -> k b c")
        E('pool', lambda e: e.dma_start(out=KC2.t[:, :, 0, :], in_=ckv), writes=[KC2.all()], dma='kc')
        E('pool', lambda e: e.dma_start(out=KC2.t[:, :, 1, 0:64], in_=ckv[:, :, 64:128]), writes=[KC2.all()], dma='kc')
        E('pool', lambda e: e.dma_start(out=KC2.t[:, :, 1, 64:128], in_=ckv[:, :, 0:64]), writes=[KC2.all()], dma='kc')
        E('pool', lambda e: e.memset(VAS.t[:, :, :], 1.0), writes=[VAS.all()])
        cvv = cv.rearrange("b k (a c) -> k b a c", c=64)
        for a in range(2):
            E('pool', lambda e: e.dma_start(out=VAS.t[:, :, 64 + 128 * a:128 + 128 * a], in_=cvv[:, :, a, :]),
              writes=[VAS.all()], dma='vc')
        E('pool', lambda e: e.memset(KCT.t[:, :, :, :], 0.0), writes=[KCT.all(), ('KCTb', 0, 4)])
        E('pool', lambda e: e.memset(OWN.t[:, :, :, :, :], -240000.0), writes=[OWN.all(), ('OWNd', 0, 64)])
        for b in range(16):
            for v in range(4):
                src = bass.AP(Bscr_t, 127 + v * 4 * 128 * 384, [[383, 4], [128 * 384, 4], [1, 4]])
                E('sp', lambda e: e.dma_start(out=OWN.t[64 + 4 * b:68 + 4 * b, v, b, :, :], in_=src),
                  reads=[('Bscr', 0, 2)], writes=[('OWNd', b * 4 + v, b * 4 + v + 1)], dma='own')

    units = [(i, v) for i in range(1, 17) for v in range(4)]
    NU = len(units)

    def qk_unit(idx):
        i, v = units[idx]
        g = v // 2
        q0, q1 = blocks[i]
        sb0 = (2 * idx) % 4
        pb0 = (2 * idx) % 4
        for kb in range(2):
            sb_ = sb0 + kb
            k0, k1 = blocks[i - 1 + kb]
            ti = (2 if i == 1 else 1) if kb == 0 else 0
            mm(ps[:, sb_, 0:512], IDB.t[:, :], TB.t[:, ti, 4 * v:4 * v + 4, :], True, False,
               [IDB.all(), TB.iv((ti, 4 * v), 0, 512)], [psiv(sb_, 0, 512)])
            mm(ps[:, sb_, 0:512], KP.t[:, v, k0:k1],
               SB.t[:, 4 * g:4 * g + 4, q0:q1], False, True,
               [KP.iv((v,), k0, k1)] + [SB.iv((c,), q0, q1) for c in range(4 * g, 4 * g + 4)],
               [psiv(sb_, 0, 512)])
        E('act', lambda e: e.activation(out=PB.t[:, pb0:pb0 + 2, :], in_=ps[:, sb0:sb0 + 2, 0:512], func=AF.Exp, scale=0.125),
          reads=[psiv(sb0, 0, 512), psiv(sb0 + 1, 0, 512)], writes=[PB.iv((pb0,), 0, 1024)])

    def pv_unit(idx):
        i, v = units[idx]
        g, p = v // 2, v % 2
        q0, q1 = blocks[i]
        ob = 4 + (idx % 2)
        a0, a1 = VSL[v]
        for kb in range(2):
            sb_ = (2 * idx) % 4 + kb
            mm(ps[:, ob, 0:512], VA.t[:, i - 1 + kb, a0:a1], PB.t[:, sb_, :], kb == 0, False,
               [VA.iv((i - 1 + kb,), 0, 320), PB.iv((sb_,), 0, 512)], [psiv(ob, 0, 512)])
        mm(ps[:, ob, 0:512], ESEL.t[:, p, :], ESR.t[:, 4 * v:4 * v + 4, :],
           False, True, [ESEL.all(), ESR.all()], [psiv(ob, 0, 512)])
        vh = slice(0, 64) if p == 0 else slice(64, 128)
        dh = slice(64, 128) if p == 0 else slice(0, 64)
        rs = idx % 2
        CS = 416
        E('act', lambda e: e.activation(out=REC.t[vh, rs, 0:CS], in_=ps[dh, ob, 0:CS], func=AF.Ln),
          reads=[psiv(ob, 0, 512)], writes=[REC.iv((rs,), 0, CS), ('LNord', rs, rs + 1)])
        E('dve', lambda e: e.reciprocal(out=REC.t[vh, rs, CS:512], in_=ps[dh, ob, CS:512]),
          reads=[psiv(ob, 0, 512), ('LNord', rs, rs + 1)], writes=[REC.iv((rs,), CS, 512)])
        E('act', lambda e: e.activation(out=REC.t[vh, rs, 0:CS], in_=REC.t[vh, rs, 0:CS], func=AF.Exp, scale=-1.0),
          reads=[REC.iv((rs,), 0, CS)], writes=[REC.iv((rs,), 0, CS)])
        E('dve', lambda e: e.tensor_tensor(
            out=SB.t[vh, 4 * g:4 * g + 4, q0:q1], in0=ps[vh, ob, 0:512].rearrange("p (h q) -> p h q", q=128),
            in1=REC.t[vh, rs, :].rearrange("p (h q) -> p h q", q=128), op=ALU.mult),
          reads=[psiv(ob, 0, 512), REC.iv((rs,), 0, 512)],
          writes=[SB.iv((c,), q0, q1) for c in range(4 * g, 4 * g + 4)])

    SC0 = C_S0
    w0, w1 = blocks[17]
    BX, BS = 6, 7
    PPs = (PP0, PP1)

    def sample_steps():
        def kt_step(b):
            ks = b % 4
            for a, var in enumerate((0, 1, 0)):
                mm(ps[:, BX, a * 128:(a + 1) * 128], KC2.t[:, b, var, :], IDB.t[:, :], True, True,
                   [KC2.iv((b, var), 0, 128), IDB.all()], [psiv(BX, 0, 512)])
            E('dve', lambda e: e.tensor_copy(
                out=KCT.t[0:64, ks, :, :].rearrange("p (a two) c -> p a two c", two=2)[:, :, 0, :],
                in_=ps[0:64, BX, 0:256].rearrange("p (a c) -> p a c", c=128)),
              reads=[psiv(BX, 0, 256)], writes=[KCT.iv((ks, 0), 0, 512)])
            E('dve', lambda e: e.tensor_copy(
                out=KCT.t[64:128, ks, :, :].rearrange("p (a two) c -> p a two c", two=2)[:, :, 1, :],
                in_=ps[64:128, BX, 128:384].rearrange("p (a c) -> p a c", c=128)),
              reads=[psiv(BX, 128, 384)], writes=[KCT.iv((ks, 0), 0, 512)])

        def sprev_step(b):
            ks = b % 4
            bl = b % 8
            for v in range(4):
                g = v // 2
                cc = bl * 64 + v * 16
                mm(ps[:, BS, cc:cc + 16], KCT.t[:, ks, v, :],
                   SB.t[:, 4 * g:4 * g + 4, SC0 + 4 * b:SC0 + 4 * b + 4], False, (bl == 7 and v == 3),
                   [KCT.iv((ks, v), 0, 128)] + [SB.iv((c,), SC0 + 4 * b, SC0 + 4 * b + 4) for c in range(4 * g, 4 * g + 4)],
                   [psiv(BS, cc, cc + 16)])

        kt_step(0)
        yield
        for half in range(2):
            mm(ps[:, BS, 0:512], IDB.t[:, :], TB.t[:, 1, :, 0:4].unsqueeze(1).to_broadcast([128, 8, 16, 4]), True, False,
               [IDB.all(), TB.iv((1, 0), 0, 2048)], [psiv(BS, 0, 512)])
            for bl in range(8):
                b = half * 8 + bl
                if b + 1 < 16:
                    kt_step(b + 1)
                sprev_step(b)
                yield
            PPh = PPs[half]
            ppv = PPh.t[:, :].rearrange("p (b h t) -> p b h t", b=8, t=5)[:, :, :, 0:4]
            E('act', lambda e: e.activation(out=ppv, in_=ps[:, BS, 0:512].rearrange("p (b h t) -> p b h t", b=8, t=4),
                                            func=AF.Exp, scale=0.125),
              reads=[psiv(BS, 0, 512)], writes=[PPh.all()])
            yield
        E('dve', lambda e: e.tensor_copy(out=OWNB.t[:, :], in_=OWN.t[:, :, :, :, :].rearrange("p v b h t -> p (v b h t)")),
          reads=[OWN.all(), ('OWNd', 0, 64)], writes=[OWNB.all()])
        yield
        for hb in range(2):
            mm(ps[:, BX, 0:512], IDB.t[:, :], OWNB.t[:, hb * 512:(hb + 1) * 512], True, False,
               [IDB.all(), OWNB.all()], [psiv(BX, 0, 512)])
            for v in (2 * hb, 2 * hb + 1):
                g = v // 2
                cc = (v % 2) * 256
                mm(ps[:, BX, cc:cc + 256], KP.t[:, v, w0:w1],
                   SB.t[:, 4 * g:4 * g + 4, SC0:SC0 + 64].rearrange("p h (b t) -> p b h t", t=4), False, v % 2 == 1,
                   [KP.iv((v,), w0, w1)] + [SB.iv((c,), SC0, SC0 + 64) for c in range(4 * g, 4 * g + 4)],
                   [psiv(BX, cc, cc + 256)])
            E('act', lambda e: e.activation(out=PO.t[:, :], in_=ps[:, BX, 0:512], func=AF.Exp, scale=0.125),
              reads=[psiv(BX, 0, 512)], writes=[PO.all()])
            yield
            for v in (2 * hb, 2 * hb + 1):
                g, p = v // 2, v % 2
                cc = (v % 2) * 256
                a0, a1 = VSL[v]
                mm(ps[:, BS, cc:cc + 256], VA.t[:, 17, a0:a1], PO.t[:, cc:cc + 256], True, False,
                   [VA.iv((17,), 0, 320), PO.iv((), cc, cc + 256)], [psiv(BS, cc, cc + 256)])
                for b in range(16):
                    half, bl = b // 8, b % 8
                    pc = bl * 80 + v * 20
                    mm(ps[:, BS, cc + 16 * b:cc + 16 * b + 16],
                       VAS.t[:, b, a0:a1], PPs[half].t[:, pc:pc + 20].rearrange("p (h t) -> p h t", t=5)[:, :, 0:4],
                       False, False,
                       [VAS.iv((b,), 0, 320), PPs[half].iv((), pc, pc + 20)], [psiv(BS, cc, cc + 256)])
                mm(ps[:, BS, cc:cc + 256], ESEL.t[:, p, :],
                   ESR.t[:, 4 * v:4 * v + 4, 0:4].unsqueeze(1).to_broadcast([128, 16, 4, 4]), False, True,
                   [ESEL.all(), ESR.all()], [psiv(BS, cc, cc + 256)])
                vh = slice(0, 64) if p == 0 else slice(64, 128)
                dh = slice(64, 128) if p == 0 else slice(0, 64)
                E('act', lambda e: e.activation(out=RCS.t[vh, 0, :], in_=ps[dh, BS, cc:cc + 256], func=AF.Ln),
                  reads=[psiv(BS, cc, cc + 256)], writes=[RCS.iv((0,), 0, 256)])
                E('act', lambda e: e.activation(out=RCS.t[vh, 0, :], in_=RCS.t[vh, 0, :], func=AF.Exp, scale=-1.0),
                  reads=[RCS.iv((0,), 0, 256)], writes=[RCS.iv((0,), 0, 256)])
                E('dve', lambda e: e.tensor_tensor(
                    out=SB.t[vh, 4 * g:4 * g + 4, SC0:SC0 + 64].rearrange("p h (b t) -> p b h t", t=4),
                    in0=ps[vh, BS, cc:cc + 256].rearrange("p (b h t) -> p b h t", h=4, t=4),
                    in1=RCS.t[vh, 0, :].rearrange("p (b h t) -> p b h t", h=4, t=4), op=ALU.mult),
                  reads=[psiv(BS, cc, cc + 256), RCS.iv((0,), 0, 256)],
                  writes=[SB.iv((c,), SC0, SC0 + 64) for c in range(4 * g, 4 * g + 4)])
                yield

    make_tfirst()
    sample_loads()
    sgen = sample_steps()
    for idx in range(NU + 1):
        if idx < NU:
            qk_unit(idx)
        if idx >= 1:
            pv_unit(idx - 1)
        if idx >= 13 and idx % 2 == 1:
            next(sgen, None)
    for _ in sgen:
        pass

    MK['pattn_done'] = len(S.ops)
    MK['sattn_done'] = len(S.ops)
    proj_add(SB, A_q, (3, False))
    emit_ffn(A_q, (4, True))

    assert piece_ctr[0] == len(pieces), (piece_ctr[0], len(pieces))
    assert not deferred

    if marks is not None:
        marks.update(MK)
    if max_ops is not None:
        del S.ops[max_ops:]
    final_groups = ['outc'] + ['y%d_%d' % (a, k) for a in range(2) for k in range(NCH)]
    sem_keys = S.finalize(nc, final_groups)
    sems = {}
    for key in sem_keys:
        sems[key] = nc.alloc_semaphore(name="s_%s_%s" % key)
    with nc.Block() as block:
        @block.tensor
        def _(e):
            S.run(nc, sems, final_groups, ('const', 'constp', 'own'))('pe', e)

        @block.scalar
        def _(e):
            S.run(nc, sems, final_groups, ('const', 'constp', 'own'))('act', e)

        @block.vector
        def _(e):
            S.run(nc, sems, final_groups, ('const', 'constp', 'own'))('dve', e)

        @block.gpsimd
        def _(e):
            S.run(nc, sems, final_groups, ('const', 'constp', 'own'))('pool', e)

        @block.sync
        def _(e):
            S.run(nc, sems, final_groups, ('const', 'constp', 'own'))('sp', e)
    return nc


_PROG = {}


def _get_prog():
    if 'nc' not in _PROG:
        _PROG['nc'] = build_program()
    return _PROG['nc']


def make_in_maps(inputs):
    f = lambda a: np.ascontiguousarray(np.asarray(a, dtype=np.float32))
    x_prompt = f(inputs['x_prompt'])[0]
    x_sample = f(inputs['x_sample'])
    state_conv = f(inputs['state_conv'])[0]
    cache_k = f(inputs['cache_k'])[0].reshape(128, 128, 128)
    cache_v = f(inputs['cache_v'])[0].reshape(128, 128, 128)
    g_mix, g_ffn, g_final = f(inputs['g_mix']), f(inputs['g_ffn']), f(inputs['g_final'])
    gs = np.stack([g_mix[0], g_ffn[0], g_mix[1], g_ffn[1], g_final], 0)
    gall = np.ascontiguousarray(gs.reshape(5, 8, 128).transpose(2, 0, 1).reshape(128, 40))
    cw = f(inputs['conv_w'])[0]
    cwd = np.ascontiguousarray(cw.reshape(3, 8, 128).transpose(2, 0, 1).reshape(128, 24))
    rel = np.zeros((128, 16), np.float32)
    rel[:32] = f(inputs['rel_table'])
    rel[32] = -30000.0
    sinks = f(inputs['sinks']).reshape(1, 16)
    onehot = np.zeros((128, 384), np.float32)
    for i in range(383):
        dist = i - 127
        if 0 <= dist <= 128:
            onehot[int(t5_bucket_np(np.array(dist, np.int32))), i] = 1.0
        else:
            onehot[32, i] = 1.0
    ident = np.eye(128, dtype=np.float32)
    w_k = f(inputs['w_k'])[0]
    wkpad = np.zeros((4, D, 128), np.float32)
    for g in range(2):
        for p in range(2):
            wkpad[g * 2 + p][:, p * 64:(p + 1) * 64] = w_k[:, g * 64:(g + 1) * 64]
    shared = dict(
        gall=gall, cwd=cwd, sinks=sinks, rel=rel, onehot=onehot, ident=ident,
        w_conv_in=f(inputs['w_conv_in'])[0], w_conv_out=f(inputs['w_conv_out'])[0],
        w_q=f(inputs['w_q'])[0], wkpad=wkpad, w_k=w_k, w_v=f(inputs['w_v'])[0], w_o=f(inputs['w_o'])[0],
        w_gate=f(inputs['w_gate']), w_up=f(inputs['w_up']), w_down=f(inputs['w_down']),
    )
    xpad = np.concatenate([np.zeros((130, D), np.float32), x_prompt], 0)
    in_maps = []
    for c in range(NCORES):
        rows = np.concatenate([
            xpad[2048 * c:2048 * c + 2],
            xpad[2048 * c + 2:2048 * c + 130],
            x_prompt[2048 * c:2048 * (c + 1)],
            x_sample[16 * c:16 * (c + 1)].reshape(64, D),
        ], 0)
        m = dict(shared)
        m['xT'] = np.ascontiguousarray(rows.T)
        m['scT'] = np.ascontiguousarray(state_conv[16 * c:16 * (c + 1)].transpose(2, 0, 1))
        m['ck'] = np.ascontiguousarray(cache_k[16 * c:16 * (c + 1)])
        m['cv'] = np.ascontiguousarray(cache_v[16 * c:16 * (c + 1)])
        m['cmask'] = np.full((128, 1), -240000.0 if c == 0 else 0.0, np.float32)
        in_maps.append(m)
    return in_maps


def assemble(results):
    yp = np.concatenate([r['yT'][:, :2048].T for r in results], 0)[None]
    ys = np.concatenate([r['yT'][:, 2048:].T.reshape(16, 4, D) for r in results], 0)
    scp = np.ascontiguousarray(results[-1]['uT'][:, 0:2].T)[None, None]
    scs = np.concatenate([r['uT'][:, 2:66].T.reshape(16, 4, D)[:, 2:4] for r in results], 0)[None]
    ckp = results[-1]['kvp'][0].reshape(1, 1, 128, 2, 64)
    cvp = results[-1]['kvp'][1].reshape(1, 1, 128, 2, 64)
    cks = np.concatenate([r['cks'] for r in results], 0).reshape(1, 128, 128, 2, 64)
    cvs = np.concatenate([r['cvs'] for r in results], 0).reshape(1, 128, 128, 2, 64)
    out = (yp, ys, scp, scs, ckp, cks, cvp, cvs)
    return tuple(np.ascontiguousarray(o, dtype=np.float32) for o in out)


def kernel(**inputs):
    nc = _get_prog()
    in_maps = make_in_maps(inputs)
    res = run_bass_kernel_spmd(nc, in_maps, core_ids=list(range(NCORES)))
    return assemble(res.results)
```

```python
import math
import numpy as np
import concourse.bass as bass
import concourse.mybir as mybir
from concourse.bass_utils import run_bass_kernel_spmd

F32 = mybir.dt.float32
BF16 = mybir.dt.bfloat16
AF = mybir.ActivationFunctionType
ALU = mybir.AluOpType

NCORES = 8
D = 1024
NCH = 8
DFF = 2816
NFF = 22
NT = 2242
C_H0 = 2
C_P0 = 130
C_S0 = 2178
NQ = NT - C_P0
EPS = 1e-5
FF_GROUPS = [list(range(0, 8)), list(range(8, 15)), list(range(15, 22))]
NW = 8


def split(c0, c1, n=5):
    tot = c1 - c0
    base = tot // n
    rem = tot % n
    out = []
    c = c0
    for i in range(n):
        w = base + (1 if i < rem else 0)
        out.append((c, c + w))
        c += w
    return out


class Sched:
    def __init__(self):
        self.ops = []
        self.regions = {}

    def _recs(self, rg):
        return self.regions.setdefault(rg, {})

    def emit(self, eng, fn, reads=(), writes=(), dma=None):
        op = dict(id=len(self.ops), eng=eng, fn=fn, deps=set(), dma=dma, signal=False)
        for (rg, b0, b1) in reads:
            recs = self._recs(rg)
            for (k0, k1), rec in recs.items():
                if k0 < b1 and b0 < k1 and rec[0] is not None:
                    op['deps'].add(rec[0])
            rec = recs.get((b0, b1))
            if rec is None:
                rec = [None, {}]
                recs[(b0, b1)] = rec
            rec[1][eng if dma is None else ('dma', op['id'])] = op['id']
        for (rg, b0, b1) in writes:
            recs = self._recs(rg)
            dele = []
            for (k0, k1), rec in recs.items():
                if k0 < b1 and b0 < k1:
                    if rec[0] is not None:
                        op['deps'].add(rec[0])
                    for r in rec[1].values():
                        op['deps'].add(r)
                    if b0 <= k0 and k1 <= b1:
                        dele.append((k0, k1))
            for k in dele:
                del recs[k]
            recs[(b0, b1)] = [op['id'], {}]
        op['deps'].discard(op['id'])
        self.ops.append(op)
        return op

    def finalize(self, nc, final_groups):
        ops = self.ops
        engs = ['pe', 'act', 'dve', 'pool', 'sp']
        for op in ops:
            for d in op['deps']:
                dop = ops[d]
                if dop['dma'] is not None:
                    continue
                if dop['eng'] == 'pe' and op['eng'] == 'pe' and op['dma'] is None:
                    continue
                dop['signal'] = True
        cnt = {e: 0 for e in engs}
        gcnt = {}
        for op in ops:
            if op['dma'] is not None:
                g = op['dma']
                gcnt[g] = gcnt.get(g, 0) + 16
                op['sig'] = (('g', g), gcnt[g])
            elif op['signal']:
                cnt[op['eng']] += 1
                op['sig'] = (('e', op['eng']), cnt[op['eng']])
        self.gtotal = dict(gcnt)
        sem_keys = [('e', e) for e in engs] + [('g', g) for g in gcnt]
        return sem_keys

    def run(self, nc, sems, final_groups, wait_all_groups=()):
        ops = self.ops
        per = {e: [] for e in ['pe', 'act', 'dve', 'pool', 'sp']}
        for op in ops:
            per[op['eng']].append(op)
        gtotal = self.gtotal

        def body(eng, h):
            waited = {}
            for op in per[eng]:
                need = {}
                for d in op['deps']:
                    dop = ops[d]
                    if dop['dma'] is None:
                        if dop['eng'] == 'pe' and eng == 'pe' and op['dma'] is None:
                            continue
                    key, val = dop['sig']
                    if key[0] == 'g' and key[1] in wait_all_groups:
                        val = gtotal[key[1]]
                    if need.get(key, 0) < val:
                        need[key] = val
                for key, val in need.items():
                    if waited.get(key, 0) >= val:
                        continue
                    h.wait_ge(sems[key], val)
                    waited[key] = val
                ins = op['fn'](h)
                if op['dma'] is not None:
                    ins.then_inc(sems[op['sig'][0]], 16)
                elif op['signal']:
                    ins.then_inc(sems[op['sig'][0]], 1)
            if eng == 'sp':
                for g in final_groups:
                    if g in gtotal:
                        h.wait_ge(sems[('g', g)], gtotal[g])
        return body


class Buf:
    def __init__(self, nc, name, fshape, dtype, region, roff, boff):
        self.t = nc.alloc_sbuf_tensor_at(name, [128] + list(fshape), dtype, offset=roff + boff)
        self.es = 4 if dtype == F32 else 2
        self.fshape = list(fshape)
        self.region = region
        self.rb = boff
        self.C = fshape[-1]
        self.nbytes = self.es * int(np.prod(fshape))

    def iv(self, planes, c0, c1):
        lin = 0
        for dsz, i in zip(self.fshape[:-1], planes):
            lin = lin * dsz + i
        b = self.rb + lin * self.C * self.es
        return (self.region, b + c0 * self.es, b + c1 * self.es)

    def all(self):
        return (self.region, self.rb, self.rb + self.nbytes)


def t5_bucket_np(dist):
    n = np.maximum(dist, 0)
    max_exact = 16
    nf = np.maximum(n, 1).astype(np.float32)
    large = max_exact + (np.log(nf / np.float32(max_exact)) / np.float32(math.log(128 / max_exact))
                         * np.float32(32 - max_exact)).astype(np.int32)
    large = np.minimum(large, 31)
    return np.where(n < max_exact, n, large)


def build_program(max_ops=None, marks=None):
    nc = bass.Bass("TRN2", target_bir_lowering=False, dynamic_dma_scratch_size=8192)
    S = Sched()
    MK = {}

    def dram_in(name, shape):
        return nc.dram_tensor(name, list(shape), F32, kind="ExternalInput").ap()

    def dram_out(name, shape):
        return nc.dram_tensor(name, list(shape), F32, kind="ExternalOutput").ap()

    xT = dram_in("xT", [D, NT])
    scT = dram_in("scT", [D, 16, 2])
    ck = dram_in("ck", [16, 128, 128])
    cv = dram_in("cv", [16, 128, 128])
    gall = dram_in("gall", [128, 40])
    cwd = dram_in("cwd", [128, 24])
    sinks = dram_in("sinks", [1, 16])
    rel = dram_in("rel", [128, 16])
    onehot = dram_in("onehot", [128, 384])
    ident_d = dram_in("ident", [128, 128])
    cmask_d = dram_in("cmask", [128, 1])
    w_conv_in = dram_in("w_conv_in", [D, 3 * D])
    w_conv_out = dram_in("w_conv_out", [D, D])
    w_q = dram_in("w_q", [D, D])
    wkpad = dram_in("wkpad", [4, D, 128])
    w_k = dram_in("w_k", [D, 128])
    w_v = dram_in("w_v", [D, 128])
    w_o = dram_in("w_o", [D, D])
    w_gate = dram_in("w_gate", [2, D, DFF])
    w_up = dram_in("w_up", [2, D, DFF])
    w_down = dram_in("w_down", [2, DFF, D])

    yT = dram_out("yT", [D, NQ])
    uT = dram_out("uT", [D, 66])
    kvp = dram_out("kvp", [2, 128, 128])
    cks = dram_out("cks", [16, 128, 128])
    cvs = dram_out("cvs", [16, 128, 128])
    Bscr_t = nc.dram_tensor("Bscr", [16, 128, 384], F32, kind="Internal")
    Bscr = Bscr_t.ap()

    base = 8320
    regions = {}
    cur = [base]

    def region(name, size):
        size = (size + 31) // 32 * 32
        regions[name] = (cur[0], size)
        cur[0] += size
        return cur[0] - size

    rX = region('X', NCH * NT * 4)
    rH = region('H', NCH * NT * 2)
    rS = region('S', NCH * NT * 2)
    rW = region('W', NW * 2048)
    rT = region('T', 18976)
    rN = region('N', 6336)
    rV = region('V', 18 * 320 * 2 + 64)
    rB = region('B', 3 * 4096)
    rC = region('C', 10496)
    assert cur[0] <= 229344, cur[0]

    def mk(name, fshape, dtype, rname, boff):
        roff = regions[rname][0]
        b = Buf(nc, name, fshape, dtype, rname, roff, boff)
        assert boff + b.nbytes <= regions[rname][1], (name, boff, b.nbytes, regions[rname])
        return b

    X = mk("X", [NCH, NT], F32, 'X', 0)
    H = mk("H", [NCH, NT], BF16, 'H', 0)
    SB = mk("SB", [NCH, NT], BF16, 'S', 0)
    WR = mk("WR", [NW, 8, 128], BF16, 'W', 0)
    UB = mk("UB", [2, 2180], BF16, 'T', 0)
    US = mk("US", [2, 16, 6], BF16, 'T', 8736)
    CSB = mk("CSB", [2, 450], F32, 'T', 9120)
    BSB = mk("BSB", [3, 450], F32, 'T', 12736)
    KP = mk("KP", [4, NT], BF16, 'T', 0)
    SQ = mk("SQ", [5, 450], BF16, 'N', 0)
    TMP = mk("TMP", [1, 450], F32, 'N', 4512)
    VA = mk("VA", [18, 320], BF16, 'V', 0)
    ACC = mk("ACC", [3, 450], F32, 'V', 0)
    SG = mk("SG", [3, 450], F32, 'V', 6144)
    TB = mk("TB", [3, 16, 128], BF16, 'B', 0)
    off = [0]

    def cmk(name, fshape, dtype):
        b = mk(name, fshape, dtype, 'C', off[0])
        off[0] += (b.nbytes + 31) // 32 * 32
        return b

    GA = cmk("GA", [40], F32)
    CW = cmk("CW", [24], F32)
    ONES = cmk("ONES", [128], BF16)
    IDB = cmk("IDB", [128], BF16)
    ESEL = cmk("ESEL", [2, 128], BF16)
    ESR = cmk("ESR", [16, 128], BF16)
    CM = cmk("CM", [1], F32)
    U32 = cmk("U32", [NCH, 66], F32)
    SCB = cmk("SCB", [NCH, 16, 2], BF16)
    RELT = cmk("RELT", [16], F32)
    EF = cmk("EF", [16], F32)
    EHI = cmk("EHI", [16], BF16)
    ELO = cmk("ELO", [16], BF16)
    SNK = cmk("SNK", [16], F32)
    ESN = cmk("ESN", [16], F32)
    KST = cmk("KST", [2, 128], F32)
    VST = cmk("VST", [2, 128], F32)
    EBC = mk("EBC", [2, 16, 128], BF16, 'S', 13472)
    REP = mk("REP", [8, 384], F32, 'S', 21664)
    OHB = mk("OHB", [384], BF16, 'S', 33952)
    PB = mk("PB", [4, 512], BF16, 'H', 0)
    REC = mk("REC", [2, 512], F32, 'H', 6144)
    KC2 = mk("KC2", [16, 2, 128], BF16, 'H', 10240)
    VAS = mk("VAS", [16, 320], BF16, 'H', 18432)
    KCT = mk("KCT", [4, 4, 128], BF16, 'N', 2048)
    OWN = mk("OWN", [4, 16, 4, 4], F32, 'H', 30720)
    PP0 = mk("PP0", [640], BF16, 'H', 4096)
    PP1 = mk("PP1", [640], BF16, 'H', 28672)
    PO = mk("PO", [512], BF16, 'H', 34816)
    RCS = mk("RCS", [1, 256], F32, 'H', 30720)
    YST = mk("YST", [2, NCH, 424], F32, 'H', 0)
    OWNB = mk("OWNB", [1024], BF16, 'N', 0)

    ps_t = nc.alloc_psum_tensor("ps", [128, 8, 512], F32)

    def psiv(bank, c0, c1):
        return ('ps', bank * 2048, (bank + 1) * 2048)

    bank_ctr = [0]

    def nb():
        b = bank_ctr[0] % 8
        bank_ctr[0] += 1
        return b

    ps = ps_t

    pieces = []

    def wview(ap2d, col0):
        return ap2d.rearrange("(k p) n -> p k n", p=128)[:, :, col0:col0 + 128]

    wstate = dict(loaded=0)

    def add_piece(src, npl=8):
        pieces.append((src, npl))
        return len(pieces) - 1

    def ensure_loaded(upto):
        upto = min(upto, len(pieces) - 1)
        while wstate['loaded'] <= upto:
            n = wstate['loaded']
            src, npl = pieces[n]
            slot = n % NW
            dst = WR.t[:, slot, 0:npl, :]
            S.emit('pool', (lambda d, s: (lambda e: e.dma_start(out=d, in_=s)))(dst, src),
                   writes=[WR.iv((slot, 0), 0, npl * 128)], dma='w%d' % slot)
            wstate['loaded'] += 1

    piece_ctr = [0]

    def use_piece():
        n = piece_ctr[0]
        piece_ctr[0] += 1
        assert n < wstate['loaded'], (n, wstate['loaded'])
        return n % NW

    def release_upto(n):
        ensure_loaded(n + NW)

    def release_all():
        ensure_loaded(piece_ctr[0] - 1 + NW)

    for m in range(NCH):
        add_piece(wview(w_conv_in, D + m * 128))
        add_piece(wview(w_conv_in, 2 * D + m * 128))
        add_piece(wview(w_conv_in, m * 128))
    for m in range(NCH):
        add_piece(wview(w_conv_out, m * 128))

    def ffn_pieces(l):
        for grp in FF_GROUPS:
            for j in grp:
                add_piece(wview(w_gate[l], j * 128))
                add_piece(wview(w_up[l], j * 128))
            for m in range(NCH):
                src = w_down[l][grp[0] * 128:(grp[-1] + 1) * 128, m * 128:(m + 1) * 128].rearrange(
                    "(jj p) c -> p jj c", p=128)
                add_piece(src, len(grp))
    ffn_pieces(0)
    for v in range(4):
        add_piece(wview(wkpad[v], 0))
    add_piece(wview(w_v, 0))
    add_piece(wview(w_k, 0))
    for m in range(NCH):
        add_piece(wview(w_q, m * 128))
    for m in range(NCH):
        add_piece(wview(w_o, m * 128))
    ffn_pieces(1)

    class _Rec:
        def __getattr__(self, name):
            def f(*a, **k):
                self.call = (name, a, k)
                return self
            return f

    def E(eng, fn, reads=(), writes=(), dma=None):
        r = _Rec()
        fn(r)
        name, a, k = r.call
        return S.emit(eng, lambda h: getattr(h, name)(*a, **k), reads, writes, dma)

    def mm(out, lhsT, rhs, start, stop, reads, writes):
        E('pe', lambda e: e.matmul(out, lhsT=lhsT, rhs=rhs, start=start, stop=stop), reads, writes)

    def cload(dst_buf, dst_ap, src_ap, eng='sp'):
        E(eng, lambda e: e.dma_start(out=dst_ap, in_=src_ap), writes=[dst_buf.all()],
          dma='const' if eng == 'sp' else 'constp')

    xv = xT.rearrange("(k p) c -> p k c", p=128)
    T_in = [(0, 450), (450, 898), (898, 1346), (1346, 1794), (1794, 2242)]
    def xload(si):
        c0, c1 = T_in[si]
        E('sp', lambda e: e.dma_start(out=X.t[:, :, c0:c1], in_=xv[:, :, c0:c1]),
          reads=([('XORD', si - 2, si - 1)] if si >= 2 else []),
          writes=[X.iv((k,), c0, c1) for k in range(NCH)] + [('XORD', si, si + 1)], dma='x%d' % si)

    xload(0)
    xload(1)
    cload(GA, GA.t[:, :], gall[:, :])
    cload(CW, CW.t[:, :], cwd[:, :])
    cload(CM, CM.t[:, :], cmask_d[:, :])
    cload(RELT, RELT.t[:, :], rel[:, :])
    cload(SNK, SNK.t[0:1, :], sinks[:, :])
    cload(IDB, IDB.t[:, :], ident_d[:, :], 'pool')
    cload(OHB, OHB.t[:, :], onehot[:, :], 'pool')
    cload(SCB, SCB.t[:, :, :, :], scT.rearrange("(k p) b t -> p k b t", p=128), 'pool')
    for si_ in range(2, len(T_in)):
        xload(si_)
    ensure_loaded(NW - 1)

    E('dve', lambda e: e.memset(ONES.t[:, :], 1.0), writes=[ONES.all()])
    E('act', lambda e: e.activation(out=SQ.t[:, 4, 0:2], in_=ONES.t[:, 0:2], func=AF.Square),
      reads=[ONES.all()], writes=[SQ.iv((4,), 0, 2)])

    def build_consts():
        E('dve', lambda e: e.memset(ESEL.t[:, :, :], 0.0), writes=[ESEL.all()])
        E('dve', lambda e: e.memset(ESEL.t[0:1, 0, 64:128], 1.0), writes=[ESEL.all()])
        E('dve', lambda e: e.memset(ESEL.t[0:1, 1, 0:64], 1.0), writes=[ESEL.all()])
        E('dve', lambda e: e.memset(ESR.t[:, :, :], 0.0), writes=[ESR.all()])
        E('act', lambda e: e.activation(out=ESN.t[0:1, :], in_=SNK.t[0:1, :], func=AF.Exp),
          reads=[SNK.all()], writes=[ESN.all()])
        for v in range(4):
            g, p = v // 2, v % 2
            src = ESN.t[0:1, :].rearrange("o (g hq q) -> o g hq q", g=2, hq=4, q=2)[:, g, :, p]
            E('dve', (lambda vv, s: (lambda e: e.tensor_copy(
                out=ESR.t[0:1, vv * 4:(vv + 1) * 4, :], in_=s.unsqueeze(2).to_broadcast([1, 4, 128]))))(v, src),
              reads=[ESN.all()], writes=[ESR.all()])
        MK['startup_done'] = len(S.ops)

    def bias_prep_dve():
        E('act', lambda e: e.activation(out=EF.t[:, :], in_=RELT.t[:, :], func=AF.Copy, scale=8.0),
          reads=[RELT.all()], writes=[EF.all()])
        E('dve', lambda e: e.tensor_copy(out=EHI.t[:, :], in_=EF.t[:, :]), reads=[EF.all()], writes=[EHI.all()])
        E('dve', lambda e: e.tensor_tensor(out=ELO.t[:, :], in0=EF.t[:, :], in1=EHI.t[:, :], op=ALU.subtract),
          reads=[EF.all(), EHI.all()], writes=[ELO.all()])
        for xi_, EB in enumerate((EHI, ELO)):
            for v in range(4):
                g, p = v // 2, v % 2
                src = EB.t[:, :].rearrange("o (g hq q) -> o g hq q", g=2, hq=4, q=2)[:, g, :, p]
                E('dve', lambda e: e.tensor_copy(
                    out=EBC.t[:, xi_, v * 4:(v + 1) * 4, :], in_=src.unsqueeze(2).to_broadcast([128, 4, 128])),
                  reads=[EB.all()], writes=[EBC.iv((xi_, v * 4), 0, 512)])

    def bias_items():
        for hq in range(16):
            bk = nb()
            mm(ps[:, bk, 0:384], EBC.t[:, 0, hq, :], OHB.t[:, :], True, False,
               [EBC.iv((0, hq), 0, 128), OHB.all()], [psiv(bk, 0, 384)])
            mm(ps[:, bk, 0:384], EBC.t[:, 1, hq, :], OHB.t[:, :], False, True,
               [EBC.iv((1, hq), 0, 128), OHB.all()], [psiv(bk, 0, 384)])
            E('act', lambda e: e.activation(out=REP.t[:, hq % 8, :], in_=ps[:, bk, 0:384], func=AF.Copy),
              reads=[psiv(bk, 0, 384)], writes=[REP.iv((hq % 8,), 0, 384)])
            if hq % 8 == 7:
                hb = hq // 8
                E('sp', lambda e: e.dma_start(out=Bscr.rearrange("h p c -> p h c")[:, hb * 8:(hb + 1) * 8, :],
                                              in_=REP.t[:, :, :]),
                  reads=[REP.all()], writes=[('Bscr', hb, hb + 1)], dma='bscr')
            yield
        MK['bias_done'] = len(S.ops)

    def load_bias_tiles():
        for ti, offv in ((0, 127), (1, 255)):
            src = bass.AP(Bscr_t, offv, [[383, 128], [128 * 384, 16], [1, 128]])
            E('pool', (lambda t_, s: (lambda e: e.dma_start(out=TB.t[:, t_, :, :], in_=s)))(ti, src),
              reads=[('Bscr', 0, 2)], writes=[TB.iv((ti, 0), 0, 2048)], dma='tb%d' % ti)

    def make_tfirst():
        E('dve', lambda e: e.tensor_scalar(out=TB.t[:, 2, :, :], in0=TB.t[:, 1, :, :], scalar1=CM.t[:, 0:1],
                                            scalar2=None, op0=ALU.add),
          reads=[TB.iv((1, 0), 0, 2048), CM.all()], writes=[TB.iv((2, 0), 0, 2048)])


    sq_ctr = [0]
    tmp_ctr = [0]

    def emit_norm(ni, tiles, final=False):
        for ti, (c0, c1) in enumerate(tiles):
            norm_tile(ni, ti, c0, c1, final)

    def norm_tile(ni, ti, c0, c1, final=False):
        if True:
            n = c1 - c0
            bk = nb()
            for k in range(NCH):
                sl = sq_ctr[0] % 5
                sq_ctr[0] += 1
                E('act', (lambda k_, s_: (lambda e: e.activation(out=SQ.t[:, s_, 0:n], in_=X.t[:, k_, c0:c1],
                                                                   func=AF.Square)))(k, sl),
                  reads=[X.iv((k,), c0, c1)], writes=[SQ.iv((sl,), 0, n)])
                mm(ps[:, bk, 0:n], ONES.t[:, :], SQ.t[:, sl, 0:n], k == 0, k == NCH - 1,
                   [ONES.all(), SQ.iv((sl,), 0, n)], [psiv(bk, 0, n)])
            ts = 0
            tmp_ctr[0] += 1
            E('act', lambda e: e.activation(out=TMP.t[:, ts, 0:n], in_=ps[:, bk, 0:n], func=AF.Ln,
                                            scale=1.0 / D, bias=EPS),
              reads=[psiv(bk, 0, n)], writes=[TMP.iv((ts,), 0, n)])
            E('act', lambda e: e.activation(out=ps[:, bk, 0:n], in_=TMP.t[:, ts, 0:n], func=AF.Exp, scale=-0.5),
              reads=[TMP.iv((ts,), 0, n)], writes=[psiv(bk, 0, n)])
            if not final:
                for k in range(NCH):
                    E('dve', (lambda k_: (lambda e: e.scalar_tensor_tensor(
                        out=H.t[:, k_, c0:c1], in0=X.t[:, k_, c0:c1], scalar=GA.t[:, ni * 8 + k_:ni * 8 + k_ + 1],
                        in1=ps[:, bk, 0:n], op0=ALU.mult, op1=ALU.mult)))(k),
                      reads=[X.iv((k,), c0, c1), GA.all(), psiv(bk, 0, n)], writes=[H.iv((k,), c0, c1)])
            else:
                ys = ti % 2
                yv = yT.rearrange("(k p) c -> p k c", p=128)
                for k in range(NCH):
                    E('dve', lambda e: e.scalar_tensor_tensor(
                        out=YST.t[:, ys, k, 0:n], in0=X.t[:, k, c0:c1], scalar=GA.t[:, ni * 8 + k:ni * 8 + k + 1],
                        in1=ps[:, bk, 0:n], op0=ALU.mult, op1=ALU.mult),
                      reads=[X.iv((k,), c0, c1), GA.all(), psiv(bk, 0, n)], writes=[YST.iv((ys, k), 0, n)])
                    E('sp', lambda e: e.dma_start(out=yv[:, k, c0 - C_P0:c1 - C_P0], in_=YST.t[:, ys, k, 0:n]),
                      reads=[YST.iv((ys, k), 0, n)], dma='y%d_%d' % (ys, k))

    norm_tile(0, 0, T_in[0][0], T_in[0][1])
    build_consts()
    bias_prep_dve()
    bgen = bias_items()
    norm_tile(0, 1, T_in[1][0], T_in[1][1])

    MK['norm0_done'] = len(S.ops)
    csb_ctr = [0]
    bsb_ctr = [0]
    acc_ctr = [0]
    pending = [None]

    def conv_step(m, c0, c1, bs):
        z0 = max(c0, 2)
        z1 = c1
        zt1 = min(z1, C_S0)
        nt = zt1 - z0
        ub = m % 2
        bsl = acc_ctr[0] % 3
        acc_ctr[0] += 1
        nz = z1 - z0
        for j in range(3):
            wj = CW.t[:, j * 8 + m:j * 8 + m + 1]
            src = UB.t[:, ub, z0 - 2 + j:zt1 - 2 + j]
            if j == 0:
                E('dve', lambda e: e.tensor_scalar(out=ACC.t[:, bsl, 0:nt], in0=src, scalar1=wj, scalar2=None,
                                                   op0=ALU.mult),
                  reads=[UB.iv((ub,), z0 - 2 + j, zt1 - 2 + j), CW.all()], writes=[ACC.iv((bsl,), 0, nt)])
            else:
                E('dve', lambda e: e.scalar_tensor_tensor(out=ACC.t[:, bsl, 0:nt], in0=src, scalar=wj,
                                                          in1=ACC.t[:, bsl, 0:nt], op0=ALU.mult, op1=ALU.add),
                  reads=[UB.iv((ub,), z0 - 2 + j, zt1 - 2 + j), CW.all(), ACC.iv((bsl,), 0, nt)],
                  writes=[ACC.iv((bsl,), 0, nt)])
        if z1 > C_S0:
            acc = ACC.t[:, bsl, nt:nt + 64].rearrange("p (b t) -> p b t", t=4)
            for j in range(3):
                wj = CW.t[:, j * 8 + m:j * 8 + m + 1]
                src = US.t[:, ub, :, j:j + 4]
                if j == 0:
                    E('dve', lambda e: e.tensor_scalar(out=acc, in0=src, scalar1=wj, scalar2=None, op0=ALU.mult),
                      reads=[US.iv((ub, 0), 0, 96), CW.all()], writes=[ACC.iv((bsl,), nt, nt + 64)])
                else:
                    E('dve', lambda e: e.scalar_tensor_tensor(out=acc, in0=src, scalar=wj, in1=acc,
                                                              op0=ALU.mult, op1=ALU.add),
                      reads=[US.iv((ub, 0), 0, 96), CW.all(), ACC.iv((bsl,), nt, nt + 64)],
                      writes=[ACC.iv((bsl,), nt, nt + 64)])
        E('dve', lambda e: e.tensor_tensor(out=SB.t[:, m, z0:z1], in0=ACC.t[:, bsl, 0:nz],
                                           in1=BSB.t[:, bs, z0 - c0:z1 - c0], op=ALU.mult),
          reads=[ACC.iv((bsl,), 0, nz), BSB.iv((bs,), z0 - c0, z1 - c0)], writes=[SB.iv((m,), z0, z1)])

    for mgrp in ((0, 1), (2,), (3,), (4,), (5,), (6,), (7,)):
        sls = {}
        for m in mgrp:
            sls[m] = (use_piece(), use_piece(), use_piece())
            E('dve', lambda e: e.tensor_copy(out=US.t[:, m % 2, :, 0:2], in_=SCB.t[:, m, :, :]),
              reads=[SCB.all()], writes=[US.iv((m % 2, 0), 0, 96)])
        for si, (c0, c1) in enumerate(T_in):
            if mgrp == (2,):
                for _ in range(3):
                    next(bgen, None)
            if mgrp == (3,) and si == 0:
                next(bgen, None)
            if mgrp == (0, 1) and si + 2 < len(T_in):
                norm_tile(0, si + 2, T_in[si + 2][0], T_in[si + 2][1])
            for m in mgrp:
                ub = m % 2
                n = c1 - c0
                bc, bx, bb = nb(), nb(), nb()
                for (bk, sl) in ((bc, sls[m][0]), (bx, sls[m][1]), (bb, sls[m][2])):
                    for k in range(NCH):
                        mm(ps[:, bk, 0:n], WR.t[:, sl, k, :], H.t[:, k, c0:c1], k == 0, k == NCH - 1,
                           [WR.iv((sl, k), 0, 128), H.iv((k,), c0, c1)], [psiv(bk, 0, n)])
                cs = csb_ctr[0] % 2
                csb_ctr[0] += 1
                E('act', lambda e: e.activation(out=CSB.t[:, cs, 0:n], in_=ps[:, bc, 0:n], func=AF.Copy),
                  reads=[psiv(bc, 0, n)], writes=[CSB.iv((cs,), 0, n)])
                t1 = min(c1, C_S0)
                E('dve', lambda e: e.tensor_tensor(out=UB.t[:, ub, c0:t1], in0=CSB.t[:, cs, 0:t1 - c0],
                                                   in1=ps[:, bx, 0:t1 - c0], op=ALU.mult),
                  reads=[CSB.iv((cs,), 0, t1 - c0), psiv(bx, 0, t1 - c0)], writes=[UB.iv((ub,), c0, t1)])
                if c1 > C_S0:
                    o0 = C_S0 - c0
                    E('dve', lambda e: e.tensor_tensor(
                        out=US.t[:, ub, :, 2:6], in0=CSB.t[:, cs, o0:o0 + 64].rearrange("p (b t) -> p b t", t=4),
                        in1=ps[:, bx, o0:o0 + 64].rearrange("p (b t) -> p b t", t=4), op=ALU.mult),
                      reads=[CSB.iv((cs,), o0, o0 + 64), psiv(bx, o0, o0 + 64)], writes=[US.iv((ub, 0), 0, 96)])
                    o1 = C_S0 - 2 - c0
                    E('dve', lambda e: e.tensor_tensor(out=U32.t[:, m, :], in0=CSB.t[:, cs, o1:o1 + 66],
                                                       in1=ps[:, bx, o1:o1 + 66], op=ALU.mult),
                      reads=[CSB.iv((cs,), o1, o1 + 66), psiv(bx, o1, o1 + 66)], writes=[U32.iv((m,), 0, 66)])
                bs = bsb_ctr[0] % 3
                bsb_ctr[0] += 1
                E('act', lambda e: e.activation(out=BSB.t[:, bs, 0:n], in_=ps[:, bb, 0:n], func=AF.Copy),
                  reads=[psiv(bb, 0, n)], writes=[BSB.iv((bs,), 0, n)])
                if pending[0] is not None:
                    conv_step(*pending[0])
                pending[0] = (m, c0, c1, bs)
        release_all()
    conv_step(*pending[0])
    pending[0] = None
    E('sp', lambda e: e.dma_start(out=cks[:, 0:124, :], in_=ck[:, 4:128, :]), dma='outc')
    E('sp', lambda e: e.dma_start(out=cvs[:, 0:124, :], in_=cv[:, 4:128, :]), dma='outc')
    for _ in bgen:
        pass
    load_bias_tiles()
    uv = uT.rearrange("(k p) c -> p k c", p=128)
    E('sp', lambda e: e.dma_start(out=uv[:, :, :], in_=U32.t[:, :, :]), reads=[U32.all()], dma='outc')

    MK['convin_done'] = len(S.ops)
    A_x = split(C_H0, NT)
    A_q = split(C_P0, NT)

    deferred = []

    def run_deferred():
        while deferred:
            deferred.pop(0)()

    def proj_add(src_buf, tiles, next_norm):
        n0 = piece_ctr[0]
        slots = [use_piece() for _ in range(NCH)]
        ni, fin = next_norm
        for si, (c0, c1) in enumerate(tiles):
            n = c1 - c0
            for m in range(NCH):
                sl = slots[m]
                bk = nb()
                for k in range(NCH):
                    mm(ps[:, bk, 0:n], WR.t[:, sl, k, :], src_buf.t[:, k, c0:c1], k == 0, k == NCH - 1,
                       [WR.iv((sl, k), 0, 128), src_buf.iv((k,), c0, c1)], [psiv(bk, 0, n)])
                E('dve', lambda e: e.tensor_tensor(out=X.t[:, m, c0:c1], in0=X.t[:, m, c0:c1],
                                                   in1=ps[:, bk, 0:n], op=ALU.add),
                  reads=[X.iv((m,), c0, c1), psiv(bk, 0, n)], writes=[X.iv((m,), c0, c1)])
                if si == len(tiles) - 1:
                    release_upto(n0 + m)
            if si >= 1:
                norm_tile(ni, si - 1, tiles[si - 1][0], tiles[si - 1][1], fin)
        if fin:
            norm_tile(ni, len(tiles) - 1, tiles[-1][0], tiles[-1][1], fin)
        else:
            deferred.append(lambda: norm_tile(ni, len(tiles) - 1, tiles[-1][0], tiles[-1][1], fin))

    proj_add(SB, A_x, (1, False))

    MK['convout_done'] = len(S.ops)
    sg_ctr = [0]

    def emit_ffn(tiles, next_norm):
        def gu_item(jj, slg, slu, c0, c1):
            n = c1 - c0
            bg, bu = nb(), nb()
            for (bk, sl) in ((bg, slg), (bu, slu)):
                for k in range(NCH):
                    mm(ps[:, bk, 0:n], WR.t[:, sl, k, :], H.t[:, k, c0:c1], k == 0, k == NCH - 1,
                       [WR.iv((sl, k), 0, 128), H.iv((k,), c0, c1)], [psiv(bk, 0, n)])
            ss = sg_ctr[0] % 3
            sg_ctr[0] += 1
            E('act', lambda e: e.activation(out=SG.t[:, ss, 0:n], in_=ps[:, bg, 0:n], func=AF.Silu),
              reads=[psiv(bg, 0, n)], writes=[SG.iv((ss,), 0, n)])
            E('dve', lambda e: e.tensor_tensor(out=SB.t[:, jj, c0:c1], in0=SG.t[:, ss, 0:n],
                                               in1=ps[:, bu, 0:n], op=ALU.mult),
              reads=[SG.iv((ss,), 0, n), psiv(bu, 0, n)], writes=[SB.iv((jj,), c0, c1)])

        for grp in FF_GROUPS:
            jj0 = 0
            if grp is FF_GROUPS[0] and deferred:
                sl = [(use_piece(), use_piece()) for _ in range(2)]
                T = tiles
                gu_item(0, sl[0][0], sl[0][1], *T[0])
                gu_item(0, sl[0][0], sl[0][1], *T[1])
                run_deferred()
                for t in T[2:-1]:
                    gu_item(0, sl[0][0], sl[0][1], *t)
                for t in T[:-1]:
                    gu_item(1, sl[1][0], sl[1][1], *t)
                gu_item(0, sl[0][0], sl[0][1], *T[-1])
                gu_item(1, sl[1][0], sl[1][1], *T[-1])
                release_all()
                jj0 = 2
            for jj in range(jj0, len(grp)):
                sl_g = use_piece()
                sl_u = use_piece()
                for (c0, c1) in tiles:
                    gu_item(jj, sl_g, sl_u, c0, c1)
                release_all()
            ng = len(grp)
            last = grp is FF_GROUPS[-1]
            if not last:
                for m in range(NCH):
                    sl = use_piece()
                    for (c0, c1) in tiles:
                        n = c1 - c0
                        bk = nb()
                        for jj in range(ng):
                            mm(ps[:, bk, 0:n], WR.t[:, sl, jj, :], SB.t[:, jj, c0:c1], jj == 0, jj == ng - 1,
                               [WR.iv((sl, jj), 0, 128), SB.iv((jj,), c0, c1)], [psiv(bk, 0, n)])
                        E('dve', lambda e: e.tensor_tensor(out=X.t[:, m, c0:c1], in0=X.t[:, m, c0:c1],
                                                           in1=ps[:, bk, 0:n], op=ALU.add),
                          reads=[X.iv((m,), c0, c1), psiv(bk, 0, n)], writes=[X.iv((m,), c0, c1)])
                    release_all()
            else:
                n0 = piece_ctr[0]
                slots = [use_piece() for _ in range(NCH)]
                ni, fin = next_norm
                if fin:
                    lc0, lc1 = tiles[-1]
                    tiles = tiles[:-1] + [(lc0, lc1 - 120), (lc1 - 120, lc1)]
                for si, (c0, c1) in enumerate(tiles):
                    n = c1 - c0
                    for m in range(NCH):
                        sl = slots[m]
                        bk = nb()
                        for jj in range(ng):
                            mm(ps[:, bk, 0:n], WR.t[:, sl, jj, :], SB.t[:, jj, c0:c1], jj == 0, jj == ng - 1,
                               [WR.iv((sl, jj), 0, 128), SB.iv((jj,), c0, c1)], [psiv(bk, 0, n)])
                        E('dve', lambda e: e.tensor_tensor(out=X.t[:, m, c0:c1], in0=X.t[:, m, c0:c1],
                                                           in1=ps[:, bk, 0:n], op=ALU.add),
                          reads=[X.iv((m,), c0, c1), psiv(bk, 0, n)], writes=[X.iv((m,), c0, c1)])
                        if si == len(tiles) - 1:
                            release_upto(n0 + m)
                    if si >= 1:
                        norm_tile(ni, si - 1, tiles[si - 1][0], tiles[si - 1][1], fin)
                if fin:
                    norm_tile(ni, len(tiles) - 1, tiles[-1][0], tiles[-1][1], fin)
                else:
                    deferred.append(lambda: norm_tile(ni, len(tiles) - 1, tiles[-1][0], tiles[-1][1], fin))

    emit_ffn(A_x, (2, False))

    MK['ffn0_done'] = len(S.ops)
    def kp_item(v, sl, c0, c1):
        n = c1 - c0
        bk = nb()
        for k in range(NCH):
            mm(ps[:, bk, 0:n], WR.t[:, sl, k, :], H.t[:, k, c0:c1], k == 0, k == NCH - 1,
               [WR.iv((sl, k), 0, 128), H.iv((k,), c0, c1)], [psiv(bk, 0, n)])
        E('act', lambda e: e.activation(out=KP.t[:, v, c0:c1], in_=ps[:, bk, 0:n], func=AF.Copy),
          reads=[psiv(bk, 0, n)], writes=[KP.iv((v,), c0, c1)])

    slk = [use_piece() for _ in range(4)]
    kp_item(0, slk[0], *A_x[0])
    kp_item(0, slk[0], *A_x[1])
    kp_item(1, slk[1], *A_x[0])
    kp_item(1, slk[1], *A_x[1])
    run_deferred()
    for v in range(4):
        for t in (A_x[2:-1] if v < 2 else A_x[:-1]):
            kp_item(v, slk[v], *t)
    for v in range(4):
        kp_item(v, slk[v], *A_x[-1])
    release_all()
    MK['kpad_done'] = len(S.ops)
    E('dve', lambda e: e.memset(VA.t[:, :, :], 1.0), writes=[VA.all()])
    blocks = [(C_H0 + 128 * i, C_H0 + 128 * (i + 1)) for i in range(17)] + [(NT - 128, NT)]
    sl_v = use_piece()
    for bi, (b0, b1) in enumerate(blocks):
        bk = nb()
        for k in range(NCH):
            mm(ps[:, bk, 0:128], H.t[:, k, b0:b1], WR.t[:, sl_v, k, :], k == 0, k == NCH - 1,
               [WR.iv((sl_v, k), 0, 128), H.iv((k,), b0, b1)], [psiv(bk, 0, 128)])
        E('dve', (lambda bi_: (lambda e: e.tensor_copy(
            out=VA.t[:, bi_, 64:320].rearrange("p (a c) -> p a c", c=128)[:, :, 0:64],
            in_=ps[:, bk, 0:128].rearrange("p (a c) -> p a c", c=64))))(bi),
          reads=[psiv(bk, 0, 128)], writes=[VA.iv((bi,), 0, 320)])
        if bi >= 16:
            vs = bi - 16
            E('dve', lambda e: e.tensor_copy(out=VST.t[:, vs, :], in_=ps[:, bk, 0:128]),
              reads=[psiv(bk, 0, 128)], writes=[VST.iv((vs,), 0, 128)])
            if bi == 16:
                E('sp', lambda e: e.dma_start(out=kvp[1, :, :], in_=VST.t[:, 0, :]),
                  reads=[VST.iv((0,), 0, 128)], dma='outc')
            else:
                for b in range(16):
                    E('sp', (lambda b_: (lambda e: e.dma_start(out=cvs[b_, 124:128, :],
                                                                in_=VST.t[64 + 4 * b_:68 + 4 * b_, 1, :])))(b),
                      reads=[VST.iv((1,), 0, 128)], dma='outc')
    MK['vtok_done'] = len(S.ops)
    release_all()
    sl_k = use_piece()
    for bi in (16, 17):
        b0, b1 = blocks[bi]
        bk = nb()
        for k in range(NCH):
            mm(ps[:, bk, 0:128], H.t[:, k, b0:b1], WR.t[:, sl_k, k, :], k == 0, k == NCH - 1,
               [WR.iv((sl_k, k), 0, 128), H.iv((k,), b0, b1)], [psiv(bk, 0, 128)])
        ks = bi - 16
        E('act', lambda e: e.activation(out=KST.t[:, ks, :], in_=ps[:, bk, 0:128], func=AF.Copy),
          reads=[psiv(bk, 0, 128)], writes=[KST.iv((ks,), 0, 128)])
        if bi == 16:
            E('sp', lambda e: e.dma_start(out=kvp[0, :, :], in_=KST.t[:, 0, :]),
              reads=[KST.iv((0,), 0, 128)], dma='outc')
        else:
            for b in range(16):
                E('sp', (lambda b_: (lambda e: e.dma_start(out=cks[b_, 124:128, :],
                                                            in_=KST.t[64 + 4 * b_:68 + 4 * b_, 1, :])))(b),
                  reads=[KST.iv((1,), 0, 128)], dma='outc')
    MK['ktok_done'] = len(S.ops)
    release_all()
    for m in range(NCH):
        sl = use_piece()
        for (c0, c1) in A_q:
            n = c1 - c0
            bk = nb()
            for k in range(NCH):
                mm(ps[:, bk, 0:n], WR.t[:, sl, k, :], H.t[:, k, c0:c1], k == 0, k == NCH - 1,
                   [WR.iv((sl, k), 0, 128), H.iv((k,), c0, c1)], [psiv(bk, 0, n)])
            E('act', (lambda m_: (lambda e: e.activation(out=SB.t[:, m_, c0:c1], in_=ps[:, bk, 0:n], func=AF.Copy)))(m),
              reads=[psiv(bk, 0, n)], writes=[SB.iv((m,), c0, c1)])
        release_all()

    MK['qkv_done'] = len(S.ops)
    VSL = {0: (64, 192), 1: (0, 128), 2: (192, 320), 3: (128, 256)}
    def sample_loads():
        ckv = ck.rearrange("b k c -> k b c")
        E('pool', lambda e: e.dma_start(out=KC2.t[:, :, 0, :], in_=ckv), writes=[KC2.all()], dma='kc')
        E('pool', lambda e: e.dma_start(out=KC2.t[:, :, 1, 0:64], in_=ckv[:, :, 64:128]), writes=[KC2.all()], dma='kc')
        E('pool', lambda e: e.dma_start(out=KC2.t[:, :, 1, 64:128], in_=ckv[:, :, 0:64]), writes=[KC2.all()], dma='kc')
        E('pool', lambda e: e.memset(VAS.t[:, :, :], 1.0), writes=[VAS.all()])
        cvv = cv.rearrange("b k (a c) -> k b a c", c=64)
        for a in range(2):
            E('pool', lambda e: e.dma_start(out=VAS.t[:, :, 64 + 128 * a:128 + 128 * a], in_=cvv[:, :, a, :]),
              writes=[VAS.all()], dma='vc')
        E('pool', lambda e: e.memset(KCT.t[:, :, :, :], 0.0), writes=[KCT.all(), ('KCTb', 0, 4)])
        E('pool', lambda e: e.memset(OWN.t[:, :, :, :, :], -240000.0), writes=[OWN.all(), ('OWNd', 0, 64)])
        for b in range(16):
            for v in range(4):
                src = bass.AP(Bscr_t, 127 + v * 4 * 128 * 384, [[383, 4], [128 * 384, 4], [1, 4]])
                E('sp', lambda e: e.dma_start(out=OWN.t[64 + 4 * b:68 + 4 * b, v, b, :, :], in_=src),
                  reads=[('Bscr', 0, 2)], writes=[('OWNd', b * 4 + v, b * 4 + v + 1)], dma='own')

    units = [(i, v) for i in range(1, 17) for v in range(4)]
    NU = len(units)

    def qk_unit(idx):
        i, v = units[idx]
        g = v // 2
        q0, q1 = blocks[i]
        sb0 = (2 * idx) % 4
        pb0 = (2 * idx) % 4
        for kb in range(2):
            sb_ = sb0 + kb
            k0, k1 = blocks[i - 1 + kb]
            ti = (2 if i == 1 else 1) if kb == 0 else 0
            mm(ps[:, sb_, 0:512], IDB.t[:, :], TB.t[:, ti, 4 * v:4 * v + 4, :], True, False,
               [IDB.all(), TB.iv((ti, 4 * v), 0, 512)], [psiv(sb_, 0, 512)])
            mm(ps[:, sb_, 0:512], KP.t[:, v, k0:k1],
               SB.t[:, 4 * g:4 * g + 4, q0:q1], False, True,
               [KP.iv((v,), k0, k1)] + [SB.iv((c,), q0, q1) for c in range(4 * g, 4 * g + 4)],
               [psiv(sb_, 0, 512)])
        E('act', lambda e: e.activation(out=PB.t[:, pb0:pb0 + 2, :], in_=ps[:, sb0:sb0 + 2, 0:512], func=AF.Exp, scale=0.125),
          reads=[psiv(sb0, 0, 512), psiv(sb0 + 1, 0, 512)], writes=[PB.iv((pb0,), 0, 1024)])

    def pv_unit(idx):
        i, v = units[idx]
        g, p = v // 2, v % 2
        q0, q1 = blocks[i]
        ob = 4 + (idx % 2)
        a0, a1 = VSL[v]
        for kb in range(2):
            sb_ = (2 * idx) % 4 + kb
            mm(ps[:, ob, 0:512], VA.t[:, i - 1 + kb, a0:a1], PB.t[:, sb_, :], kb == 0, False,
               [VA.iv((i - 1 + kb,), 0, 320), PB.iv((sb_,), 0, 512)], [psiv(ob, 0, 512)])
        mm(ps[:, ob, 0:512], ESEL.t[:, p, :], ESR.t[:, 4 * v:4 * v + 4, :],
           False, True, [ESEL.all(), ESR.all()], [psiv(ob, 0, 512)])
        vh = slice(0, 64) if p == 0 else slice(64, 128)
        dh = slice(64, 128) if p == 0 else slice(0, 64)
        rs = idx % 2
        CS = 400
        E('act', lambda e: e.activation(out=REC.t[vh, rs, 0:CS], in_=ps[dh, ob, 0:CS], func=AF.Ln),
          reads=[psiv(ob, 0, 512)], writes=[REC.iv((rs,), 0, CS), ('LNord', rs, rs + 1)])
        E('dve', lambda e: e.reciprocal(out=REC.t[vh, rs, CS:512], in_=ps[dh, ob, CS:512]),
          reads=[psiv(ob, 0, 512), ('LNord', rs, rs + 1)], writes=[REC.iv((rs,), CS, 512)])
        E('act', lambda e: e.activation(out=REC.t[vh, rs, 0:CS], in_=REC.t[vh, rs, 0:CS], func=AF.Exp, scale=-1.0),
          reads=[REC.iv((rs,), 0, CS)], writes=[REC.iv((rs,), 0, CS)])
        E('dve', lambda e: e.tensor_tensor(
            out=SB.t[vh, 4 * g:4 * g + 4, q0:q1], in0=ps[vh, ob, 0:512].rearrange("p (h q) -> p h q", q=128),
            in1=REC.t[vh, rs, :].rearrange("p (h q) -> p h q", q=128), op=ALU.mult),
          reads=[psiv(ob, 0, 512), REC.iv((rs,), 0, 512)],
          writes=[SB.iv((c,), q0, q1) for c in range(4 * g, 4 * g + 4)])

    SC0 = C_S0
    w0, w1 = blocks[17]
    BX, BS = 6, 7
    PPs = (PP0, PP1)

    def sample_steps():
        def kt_step(b):
            ks = b % 4
            for a, var in enumerate((0, 1, 0)):
                mm(ps[:, BX, a * 128:(a + 1) * 128], KC2.t[:, b, var, :], IDB.t[:, :], True, True,
                   [KC2.iv((b, var), 0, 128), IDB.all()], [psiv(BX, 0, 512)])
            E('dve', lambda e: e.tensor_copy(
                out=KCT.t[0:64, ks, :, :].rearrange("p (a two) c -> p a two c", two=2)[:, :, 0, :],
                in_=ps[0:64, BX, 0:256].rearrange("p (a c) -> p a c", c=128)),
              reads=[psiv(BX, 0, 256)], writes=[KCT.iv((ks, 0), 0, 512)])
            E('dve', lambda e: e.tensor_copy(
                out=KCT.t[64:128, ks, :, :].rearrange("p (a two) c -> p a two c", two=2)[:, :, 1, :],
                in_=ps[64:128, BX, 128:384].rearrange("p (a c) -> p a c", c=128)),
              reads=[psiv(BX, 128, 384)], writes=[KCT.iv((ks, 0), 0, 512)])

        def sprev_step(b):
            ks = b % 4
            bl = b % 8
            for v in range(4):
                g = v // 2
                cc = bl * 64 + v * 16
                mm(ps[:, BS, cc:cc + 16], KCT.t[:, ks, v, :],
                   SB.t[:, 4 * g:4 * g + 4, SC0 + 4 * b:SC0 + 4 * b + 4], False, (bl == 7 and v == 3),
                   [KCT.iv((ks, v), 0, 128)] + [SB.iv((c,), SC0 + 4 * b, SC0 + 4 * b + 4) for c in range(4 * g, 4 * g + 4)],
                   [psiv(BS, cc, cc + 16)])

        kt_step(0)
        yield
        for half in range(2):
            mm(ps[:, BS, 0:512], IDB.t[:, :], TB.t[:, 1, :, 0:4].unsqueeze(1).to_broadcast([128, 8, 16, 4]), True, False,
               [IDB.all(), TB.iv((1, 0), 0, 2048)], [psiv(BS, 0, 512)])
            for bl in range(8):
                b = half * 8 + bl
                if b + 1 < 16:
                    kt_step(b + 1)
                sprev_step(b)
                yield
            PPh = PPs[half]
            ppv = PPh.t[:, :].rearrange("p (b h t) -> p b h t", b=8, t=5)[:, :, :, 0:4]
            E('act', lambda e: e.activation(out=ppv, in_=ps[:, BS, 0:512].rearrange("p (b h t) -> p b h t", b=8, t=4),
                                            func=AF.Exp, scale=0.125),
              reads=[psiv(BS, 0, 512)], writes=[PPh.all()])
            yield
        E('dve', lambda e: e.tensor_copy(out=OWNB.t[:, :], in_=OWN.t[:, :, :, :, :].rearrange("p v b h t -> p (v b h t)")),
          reads=[OWN.all(), ('OWNd', 0, 64)], writes=[OWNB.all()])
        yield
        for hb in range(2):
            mm(ps[:, BX, 0:512], IDB.t[:, :], OWNB.t[:, hb * 512:(hb + 1) * 512], True, False,
               [IDB.all(), OWNB.all()], [psiv(BX, 0, 512)])
            for v in (2 * hb, 2 * hb + 1):
                g = v // 2
                cc = (v % 2) * 256
                mm(ps[:, BX, cc:cc + 256], KP.t[:, v, w0:w1],
                   SB.t[:, 4 * g:4 * g + 4, SC0:SC0 + 64].rearrange("p h (b t) -> p b h t", t=4), False, v % 2 == 1,
                   [KP.iv((v,), w0, w1)] + [SB.iv((c,), SC0, SC0 + 64) for c in range(4 * g, 4 * g + 4)],
                   [psiv(BX, cc, cc + 256)])
            E('act', lambda e: e.activation(out=PO.t[:, :], in_=ps[:, BX, 0:512], func=AF.Exp, scale=0.125),
              reads=[psiv(BX, 0, 512)], writes=[PO.all()])
            yield
            for v in (2 * hb, 2 * hb + 1):
                g, p = v // 2, v % 2
                cc = (v % 2) * 256
                a0, a1 = VSL[v]
                mm(ps[:, BS, cc:cc + 256], VA.t[:, 17, a0:a1], PO.t[:, cc:cc + 256], True, False,
                   [VA.iv((17,), 0, 320), PO.iv((), cc, cc + 256)], [psiv(BS, cc, cc + 256)])
                for b in range(16):
                    half, bl = b // 8, b % 8
                    pc = bl * 80 + v * 20
                    mm(ps[:, BS, cc + 16 * b:cc + 16 * b + 16],
                       VAS.t[:, b, a0:a1], PPs[half].t[:, pc:pc + 20].rearrange("p (h t) -> p h t", t=5)[:, :, 0:4],
                       False, False,
                       [VAS.iv((b,), 0, 320), PPs[half].iv((), pc, pc + 20)], [psiv(BS, cc, cc + 256)])
                mm(ps[:, BS, cc:cc + 256], ESEL.t[:, p, :],
                   ESR.t[:, 4 * v:4 * v + 4, 0:4].unsqueeze(1).to_broadcast([128, 16, 4, 4]), False, True,
                   [ESEL.all(), ESR.all()], [psiv(BS, cc, cc + 256)])
                vh = slice(0, 64) if p == 0 else slice(64, 128)
                dh = slice(64, 128) if p == 0 else slice(0, 64)
                E('act', lambda e: e.activation(out=RCS.t[vh, 0, :], in_=ps[dh, BS, cc:cc + 256], func=AF.Ln),
                  reads=[psiv(BS, cc, cc + 256)], writes=[RCS.iv((0,), 0, 256)])
                E('act', lambda e: e.activation(out=RCS.t[vh, 0, :], in_=RCS.t[vh, 0, :], func=AF.Exp, scale=-1.0),
                  reads=[RCS.iv((0,), 0, 256)], writes=[RCS.iv((0,), 0, 256)])
                E('dve', lambda e: e.tensor_tensor(
                    out=SB.t[vh, 4 * g:4 * g + 4, SC0:SC0 + 64].rearrange("p h (b t) -> p b h t", t=4),
                    in0=ps[vh, BS, cc:cc + 256].rearrange("p (b h t) -> p b h t", h=4, t=4),
                    in1=RCS.t[vh, 0, :].rearrange("p (b h t) -> p b h t", h=4, t=4), op=ALU.mult),
                  reads=[psiv(BS, cc, cc + 256), RCS.iv((0,), 0, 256)],
                  writes=[SB.iv((c,), SC0, SC0 + 64) for c in range(4 * g, 4 * g + 4)])
                yield

    make_tfirst()
    sample_loads()
    sgen = sample_steps()
    for idx in range(NU + 1):
        if idx < NU:
            qk_unit(idx)
        if idx >= 1:
            pv_unit(idx - 1)
        if idx >= 13 and idx % 2 == 1:
            next(sgen, None)
    for _ in sgen:
        pass

    MK['pattn_done'] = len(S.ops)
    MK['sattn_done'] = len(S.ops)
    proj_add(SB, A_q, (3, False))
    emit_ffn(A_q, (4, True))

    assert piece_ctr[0] == len(pieces), (piece_ctr[0], len(pieces))
    assert not deferred

    if marks is not None:
        marks.update(MK)
    if max_ops is not None:
        del S.ops[max_ops:]
    final_groups = ['outc'] + ['y%d_%d' % (a, k) for a in range(2) for k in range(NCH)]
    sem_keys = S.finalize(nc, final_groups)
    sems = {}
    for key in sem_keys:
        sems[key] = nc.alloc_semaphore(name="s_%s_%s" % key)
    with nc.Block() as block:
        @block.tensor
        def _(e):
            S.run(nc, sems, final_groups, ('const', 'constp', 'own'))('pe', e)

        @block.scalar
        def _(e):
            S.run(nc, sems, final_groups, ('const', 'constp', 'own'))('act', e)

        @block.vector
        def _(e):
            S.run(nc, sems, final_groups, ('const', 'constp', 'own'))('dve', e)

        @block.gpsimd
        def _(e):
            S.run(nc, sems, final_groups, ('const', 'constp', 'own'))('pool', e)

        @block.sync
        def _(e):
            S.run(nc, sems, final_groups, ('const', 'constp', 'own'))('sp', e)
    return nc


_PROG = {}


def _get_prog():
    if 'nc' not in _PROG:
        _PROG['nc'] = build_program()
    return _PROG['nc']


def make_in_maps(inputs):
    f = lambda a: np.ascontiguousarray(np.asarray(a, dtype=np.float32))
    x_prompt = f(inputs['x_prompt'])[0]
    x_sample = f(inputs['x_sample'])
    state_conv = f(inputs['state_conv'])[0]
    cache_k = f(inputs['cache_k'])[0].reshape(128, 128, 128)
    cache_v = f(inputs['cache_v'])[0].reshape(128, 128, 128)
    g_mix, g_ffn, g_final = f(inputs['g_mix']), f(inputs['g_ffn']), f(inputs['g_final'])
    gs = np.stack([g_mix[0], g_ffn[0], g_mix[1], g_ffn[1], g_final], 0)
    gall = np.ascontiguousarray(gs.reshape(5, 8, 128).transpose(2, 0, 1).reshape(128, 40))
    cw = f(inputs['conv_w'])[0]
    cwd = np.ascontiguousarray(cw.reshape(3, 8, 128).transpose(2, 0, 1).reshape(128, 24))
    rel = np.zeros((128, 16), np.float32)
    rel[:32] = f(inputs['rel_table'])
    rel[32] = -30000.0
    sinks = f(inputs['sinks']).reshape(1, 16)
    onehot = np.zeros((128, 384), np.float32)
    for i in range(383):
        dist = i - 127
        if 0 <= dist <= 128:
            onehot[int(t5_bucket_np(np.array(dist, np.int32))), i] = 1.0
        else:
            onehot[32, i] = 1.0
    ident = np.eye(128, dtype=np.float32)
    w_k = f(inputs['w_k'])[0]
    wkpad = np.zeros((4, D, 128), np.float32)
    for g in range(2):
        for p in range(2):
            wkpad[g * 2 + p][:, p * 64:(p + 1) * 64] = w_k[:, g * 64:(g + 1) * 64]
    shared = dict(
        gall=gall, cwd=cwd, sinks=sinks, rel=rel, onehot=onehot, ident=ident,
        w_conv_in=f(inputs['w_conv_in'])[0], w_conv_out=f(inputs['w_conv_out'])[0],
        w_q=f(inputs['w_q'])[0], wkpad=wkpad, w_k=w_k, w_v=f(inputs['w_v'])[0], w_o=f(inputs['w_o'])[0],
        w_gate=f(inputs['w_gate']), w_up=f(inputs['w_up']), w_down=f(inputs['w_down']),
    )
    xpad = np.concatenate([np.zeros((130, D), np.float32), x_prompt], 0)
    in_maps = []
    for c in range(NCORES):
        rows = np.concatenate([
            xpad[2048 * c:2048 * c + 2],
            xpad[2048 * c + 2:2048 * c + 130],
            x_prompt[2048 * c:2048 * (c + 1)],
            x_sample[16 * c:16 * (c + 1)].reshape(64, D),
        ], 0)
        m = dict(shared)
        m['xT'] = np.ascontiguousarray(rows.T)
        m['scT'] = np.ascontiguousarray(state_conv[16 * c:16 * (c + 1)].transpose(2, 0, 1))
        m['ck'] = np.ascontiguousarray(cache_k[16 * c:16 * (c + 1)])
        m['cv'] = np.ascontiguousarray(cache_v[16 * c:16 * (c + 1)])
        m['cmask'] = np.full((128, 1), -240000.0 if c == 0 else 0.0, np.float32)
        in_maps.append(m)
    return in_maps


def assemble(results):
    yp = np.concatenate([r['yT'][:, :2048].T for r in results], 0)[None]
    ys = np.concatenate([r['yT'][:, 2048:].T.reshape(16, 4, D) for r in results], 0)
    scp = np.ascontiguousarray(results[-1]['uT'][:, 0:2].T)[None, None]
    scs = np.concatenate([r['uT'][:, 2:66].T.reshape(16, 4, D)[:, 2:4] for r in results], 0)[None]
    ckp = results[-1]['kvp'][0].reshape(1, 1, 128, 2, 64)
    cvp = results[-1]['kvp'][1].reshape(1, 1, 128, 2, 64)
    cks = np.concatenate([r['cks'] for r in results], 0).reshape(1, 128, 128, 2, 64)
    cvs = np.concatenate([r['cvs'] for r in results], 0).reshape(1, 128, 128, 2, 64)
    out = (yp, ys, scp, scs, ckp, cks, cvp, cvs)
    return tuple(np.ascontiguousarray(o, dtype=np.float32) for o in out)


def kernel(**inputs):
    nc = _get_prog()
    in_maps = make_in_maps(inputs)
    res = run_bass_kernel_spmd(nc, in_maps, core_ids=list(range(NCORES)))
    return assemble(res.results)
```
